# Optimizing a Trainium2 kernel written in Bass

```python
import math
import jax, jax.numpy as jnp
from jax import lax
import numpy as np


D_MODEL = 1024
BATCH = 8
SEQ = 4096
DEPTH = 4

N_A = DEPTH // 2
N_B = DEPTH - N_A
N_HEADS = 16
HEAD_DIM = D_MODEL // N_HEADS
CONV_W = 31
FFN_DIM = 2816
FFN_CONV_W = 3
PLE_DIM = 256
Q_BLOCK = 128
LN_EPS = 1e-5
DN_ALPHA = (2.0 * DEPTH) ** 0.25
DN_BETA = (8.0 * DEPTH) ** -0.25

kernel_name = "yoco_conformer_stickbreaking_hybrid"


def layer_norm(x, g, b):
    xf = x.astype(jnp.float32)
    mu = jnp.mean(xf, axis=-1, keepdims=True)
    var = jnp.mean(jnp.square(xf - mu), axis=-1, keepdims=True)
    y = (xf - mu) * lax.rsqrt(var + LN_EPS)
    return (y * g.astype(jnp.float32) + b.astype(jnp.float32)).astype(x.dtype)


def causal_dwconv(x, w, b):
    k = w.shape[0]
    y = lax.conv_general_dilated(
        x, w[:, None, :].astype(x.dtype), window_strides=(1,), padding=[(k - 1, 0)],
        dimension_numbers=("NWC", "WIO", "NWC"), feature_group_count=x.shape[-1])
    return y + b


def conformer_conv(x, pw1_w, pw1_b, dw_w, dw_b, ln_g, ln_b, pw2_w, pw2_b):
    h = x @ pw1_w + pw1_b
    a, g = jnp.split(h, 2, axis=-1)
    h = a * jax.nn.sigmoid(g)
    h = causal_dwconv(h, dw_w, dw_b)
    h = layer_norm(h, ln_g, ln_b)
    h = jax.nn.silu(h)
    return h @ pw2_w + pw2_b


def stick_breaking_attention(q, k, v):
    b, s, h, dh = q.shape
    nb = s // Q_BLOCK
    scale = 1.0 / math.sqrt(dh)
    kh = jnp.transpose(k, (0, 2, 1, 3))
    vh = jnp.transpose(v, (0, 2, 1, 3))
    qb = jnp.transpose(q.reshape(b, nb, Q_BLOCK, h, dh), (1, 0, 3, 2, 4))
    t0s = jnp.arange(nb, dtype=jnp.int32) * Q_BLOCK
    ts = jnp.arange(s, dtype=jnp.int32)

    def block(args):
        qblk, t0 = args
        z = jnp.einsum('bhqd,bhkd->bhqk', qblk, kh).astype(jnp.float32) * scale
        tq = t0 + jnp.arange(Q_BLOCK, dtype=jnp.int32)
        mask = ts[None, :] < tq[:, None]
        log_1m = jnp.where(mask, jax.nn.log_sigmoid(-z), 0.0)
        rev = lax.cumsum(log_1m, axis=3, reverse=True)
        log_a = jax.nn.log_sigmoid(z) + (rev - log_1m)
        a = jnp.where(mask, jnp.exp(log_a), 0.0)
        return jnp.einsum('bhqk,bhkd->bhqd', a.astype(vh.dtype), vh)

    out = lax.map(block, (qb, t0s))
    return jnp.transpose(out, (1, 0, 3, 2, 4)).reshape(b, s, h * dh)


def conv_gated_ffn(x, w_up, w_gate, conv_w, conv_b, w_down):
    u = x @ w_up
    g = causal_dwconv(x @ w_gate, conv_w, conv_b)
    return (jax.nn.silu(g) * u) @ w_down


def setup_inputs(seed: int = 0) -> dict:
    key = jax.random.key(seed)
    ks = jax.random.split(key, 32)
    D, F = D_MODEL, FFN_DIM

    def nrm(k, shape, scale):
        return jax.random.normal(k, shape, jnp.float32) * scale

    return {
        "x": nrm(ks[0], (BATCH, SEQ, D), 1.0),
        "p": nrm(ks[1], (DEPTH, BATCH, SEQ, PLE_DIM), 1.0),
        "a_pw1_w": nrm(ks[2], (N_A, D, 2 * D), D ** -0.5),
        "a_pw1_b": nrm(ks[3], (N_A, 2 * D), 0.02),
        "a_dw_w": nrm(ks[4], (N_A, CONV_W, D), CONV_W ** -0.5),
        "a_dw_b": nrm(ks[5], (N_A, D), 0.02),
        "a_ln_g": 1.0 + nrm(ks[6], (N_A, D), 0.02),
        "a_ln_b": nrm(ks[7], (N_A, D), 0.02),
        "a_pw2_w": nrm(ks[8], (N_A, D, D), D ** -0.5 * DN_BETA),
        "a_pw2_b": nrm(ks[9], (N_A, D), 0.02),
        "b_wq": nrm(ks[10], (N_B, D, D), D ** -0.5),
        "kv_wk": nrm(ks[11], (D, D), D ** -0.5),
        "kv_wv": nrm(ks[12], (D, D), D ** -0.5 * DN_BETA),
        "b_wo": nrm(ks[13], (N_B, D, D), D ** -0.5 * DN_BETA),
        "ln_mix_g": 1.0 + nrm(ks[14], (DEPTH, D), 0.02),
        "ln_mix_b": nrm(ks[15], (DEPTH, D), 0.02),
        "ffn_w_up": nrm(ks[16], (DEPTH, D, F), D ** -0.5),
        "ffn_w_gate": nrm(ks[17], (DEPTH, D, F), D ** -0.5),
        "ffn_conv_w": nrm(ks[18], (DEPTH, FFN_CONV_W, F), FFN_CONV_W ** -0.5),
        "ffn_conv_b": nrm(ks[19], (DEPTH, F), 0.02),
        "ffn_w_down": nrm(ks[20], (DEPTH, F, D), F ** -0.5 * DN_BETA),
        "ple_w_gate": nrm(ks[21], (DEPTH, D, D), D ** -0.5),
        "ple_w_proj": nrm(ks[22], (DEPTH, PLE_DIM, D), PLE_DIM ** -0.5 * DN_BETA),
        "ln_ffn_g": 1.0 + nrm(ks[23], (DEPTH, D), 0.02),
        "ln_ffn_b": nrm(ks[24], (DEPTH, D), 0.02),
    }


def reference(x, p, a_pw1_w, a_pw1_b, a_dw_w, a_dw_b, a_ln_g, a_ln_b, a_pw2_w, a_pw2_b,
              b_wq, kv_wk, kv_wv, b_wo, ln_mix_g, ln_mix_b,
              ffn_w_up, ffn_w_gate, ffn_conv_w, ffn_conv_b, ffn_w_down,
              ple_w_gate, ple_w_proj, ln_ffn_g, ln_ffn_b):
    b, s, d = x.shape
    k_shared = None
    v_shared = None
    for i in range(DEPTH):
        if i < N_A:
            mix = conformer_conv(x, a_pw1_w[i], a_pw1_b[i], a_dw_w[i], a_dw_b[i],
                                 a_ln_g[i], a_ln_b[i], a_pw2_w[i], a_pw2_b[i])
        else:
            j = i - N_A
            if k_shared is None:
                k_shared = (x @ kv_wk).reshape(b, s, N_HEADS, HEAD_DIM)
                v_shared = (x @ kv_wv).reshape(b, s, N_HEADS, HEAD_DIM)
            q = (x @ b_wq[j]).reshape(b, s, N_HEADS, HEAD_DIM)
            mix = stick_breaking_attention(q, k_shared, v_shared) @ b_wo[j]
        x = layer_norm(DN_ALPHA * x + mix, ln_mix_g[i], ln_mix_b[i])
        ffn = conv_gated_ffn(x, ffn_w_up[i], ffn_w_gate[i], ffn_conv_w[i], ffn_conv_b[i], ffn_w_down[i])
        ple = jax.nn.sigmoid(x @ ple_w_gate[i]) * (p[i] @ ple_w_proj[i])
        x = layer_norm(DN_ALPHA * x + ffn + ple, ln_ffn_g[i], ln_ffn_b[i])
    return x
```

```python
import numpy as np
import concourse.bass as bass
import concourse.mybir as mybir
from concourse.bass_utils import run_bass_kernel_spmd

F32 = mybir.dt.float32
BF16 = mybir.dt.bfloat16
F16 = mybir.dt.float16
F32R = mybir.dt.float32r
U8 = mybir.dt.uint8
AF = mybir.ActivationFunctionType
ALU = mybir.AluOpType

D = 1024
S = 4096
FF = 2816
DEPTH = 4
NA = 2
NH = 16
DH = 64
CW = 31
PLE = 256
TB = 512
NBLK = S // TB
KC = D // 128
FC = FF // 128
ALPHA = float((2.0 * DEPTH) ** 0.25)
EPS = 1e-5
NCORES = 8

ENG = ("pe", "act", "dve", "pool", "sp")


class Tok:
    __slots__ = ("sem", "val", "eng")

    def __init__(self, sem=None, val=None, eng=None):
        self.sem = sem
        self.val = val
        self.eng = eng


class Prog:
    def __init__(self, nc):
        self.nc = nc
        self.thunks = {e: [] for e in ENG}
        self.psem = {e: nc.alloc_semaphore("prog_" + e) for e in ENG}
        self.pcnt = {e: 0 for e in ENG}
        self.W = {}
        self.R = {}
        self.pe_pending = []
        self.dsem = {}

    def _deps(self, reads, writes):
        deps = []
        for r in reads:
            t = self.W.get(r)
            if t is not None:
                deps.append(t)
        for w in writes:
            t = self.W.get(w)
            if t is not None:
                deps.append(t)
            deps.extend(self.R.get(w, ()))
        return deps

    def _commit(self, tok, reads, writes):
        for r in reads:
            self.R.setdefault(r, []).append(tok)
        for w in writes:
            self.W[w] = tok
            self.R[w] = []

    def op(self, eng, fn, reads=(), writes=(), signal=None):
        if signal is None:
            signal = eng != "pe"
        deps = self._deps(reads, writes)
        if eng == "pe":
            deps = [d for d in deps if d.eng != "pe"]
        tok = Tok(eng=eng)
        if signal:
            self.pcnt[eng] += 1
            tok.sem = self.psem[eng]
            tok.val = self.pcnt[eng]
            if eng == "pe":
                for p in self.pe_pending:
                    p.sem = tok.sem
                    p.val = tok.val
                self.pe_pending = []
        else:
            self.pe_pending.append(tok)
        self._commit(tok, reads, writes)
        self.thunks[eng].append((deps, fn, self.psem[eng] if signal else None, 1))
        return tok

    def dma(self, eng, out, in_, slot, reads=(), writes=()):
        ds = self.dsem.get(slot)
        if ds is None:
            ds = [self.nc.alloc_semaphore("d_" + slot), 0]
            self.dsem[slot] = ds
        deps = self._deps(reads, writes)
        if ds[1] > 0:
            deps.append(Tok(ds[0], ds[1]))
        ds[1] += 16
        tok = Tok(ds[0], ds[1])
        self._commit(tok, reads, writes)

        def fn(e, out=out, in_=in_):
            return e.dma_start(out=out, in_=in_)
        self.thunks[eng].append((deps, fn, ds[0], 16))
        return tok

    def barrier(self):
        assert not self.pe_pending
        toks = [Tok(ds[0], ds[1]) for ds in self.dsem.values() if ds[1] > 0]
        toks += [Tok(self.psem[x], self.pcnt[x]) for x in ENG if self.pcnt[x] > 0]
        for e in ENG:
            self.thunks[e].append((list(toks), None, None, 0))
        self.W = {}
        self.R = {}

    def replay(self, block):
        names = {"pe": "tensor", "act": "scalar", "dve": "vector", "pool": "gpsimd", "sp": "sync"}
        for e in ENG:
            thunks = self.thunks[e]
            if not thunks:
                continue

            def body(h, thunks=thunks):
                waited = {}
                for deps, fn, sem, inc in thunks:
                    for d in deps:
                        assert d.sem is not None, "unresolved PE token"
                        k = id(d.sem)
                        if waited.get(k, 0) >= d.val:
                            continue
                        waited[k] = d.val
                        h.wait_ge(d.sem, d.val)
                    if fn is None:
                        continue
                    ins = fn(h)
                    if sem is not None:
                        ins.then_inc(sem, inc)
            getattr(block, names[e])(body)


def vec_layout():
    off = {}
    n = 0

    def add(key, cnt):
        nonlocal n
        off[key] = n
        n += cnt
    for l in range(NA):
        add(("pw1b", l), 16)
        add(("dwb", l), 8)
        add(("lng", l), 8)
        add(("lnb", l), 8)
        add(("pw2b", l), 8)
        add(("dww", l), CW * 8)
    for l in range(DEPTH):
        add(("mixg", l), 8)
        add(("mixb", l), 8)
        add(("fcw", l), 3 * FC)
        add(("fcb", l), FC)
        add(("ffg", l), 8)
        add(("ffb", l), 8)
    return off, n


VOFF, NV = vec_layout()
C_ONES, C_TRI, C_ID, C_ZERO, C_MASK, NCONST = 0, 128, 256, 384, 896, 896 + 2048


def host_consts():
    c = np.zeros((128, NCONST), np.float32)
    c[:, C_ONES:C_ONES + 128] = 1.0
    j = np.arange(128)[:, None]
    s = np.arange(128)[None, :]
    c[:, C_TRI:C_TRI + 128] = (j >= s).astype(np.float32)
    c[:, C_ID:C_ID + 128] = np.eye(128, dtype=np.float32)
    i = np.arange(512)[None, :]
    for d in range(4):
        c[:, C_MASK + d * 512:C_MASK + (d + 1) * 512] = ((i - 128 * d) > j).astype(np.float32)
    return c


def host_vecs(inp):
    v = np.zeros((128, NV), np.float32)

    def put(key, vec):
        vec = np.asarray(vec, np.float32)
        n = vec.shape[0] // 128
        v[:, VOFF[key]:VOFF[key] + n] = vec.reshape(n, 128).T
    for l in range(NA):
        put(("pw1b", l), inp["a_pw1_b"][l])
        put(("dwb", l), inp["a_dw_b"][l])
        put(("lng", l), inp["a_ln_g"][l])
        put(("lnb", l), inp["a_ln_b"][l])
        put(("pw2b", l), inp["a_pw2_b"][l])
        put(("dww", l), np.asarray(inp["a_dw_w"][l]).reshape(-1))
    for l in range(DEPTH):
        put(("mixg", l), inp["ln_mix_g"][l])
        put(("mixb", l), inp["ln_mix_b"][l])
        put(("fcw", l), np.asarray(inp["ffn_conv_w"][l]).reshape(-1))
        put(("fcb", l), inp["ffn_conv_b"][l])
        put(("ffg", l), inp["ln_ffn_g"][l])
        put(("ffb", l), inp["ln_ffn_b"][l])
    return v


class Arena:
    def __init__(self, ap, nbytes):
        self.ap = ap
        self.n = nbytes
        self.off = 0

    def reset(self):
        self.off = 0

    def take(self, shape, dt, esz):
        nb = esz * int(np.prod(shape[1:]))
        n = (nb + 63) // 64 * 64
        assert self.off + n <= self.n, ("arena overflow", self.off + n, self.n)
        v = self.ap[0:shape[0], self.off:self.off + nb].bitcast(dt)
        self.off += n
        if len(shape) == 3:
            v = v.rearrange("p (a b) -> p a b", a=shape[1])
        return v


def build(nlayers=DEPTH):
    nc = bass.Bass("TRN2", target_bir_lowering=False)
    dr = lambda name, shape, dt, kind: nc.dram_tensor(name, shape, dt, kind=kind).ap()
    xT_d = dr("xT", [D, S], F32, "ExternalInput")
    pT_d = dr("pT", [DEPTH, PLE, S], F32, "ExternalInput")
    vecs_d = dr("vecs", [128, NV], F32, "ExternalInput")
    consts_d = dr("consts", [128, NCONST], F32, "ExternalInput")
    wd = {}
    for name, shape in (("a_pw1_w", [NA, D, 2 * D]), ("a_pw2_w", [NA, D, D]), ("b_wq", [2, D, D]),
                        ("kv_wk", [D, D]), ("kv_wv", [D, D]), ("b_wo", [2, D, D]),
                        ("ffn_w_up", [DEPTH, D, FF]), ("ffn_w_gate", [DEPTH, D, FF]),
                        ("ffn_w_down", [DEPTH, FF, D]), ("ple_w_gate", [DEPTH, D, D]),
                        ("ple_w_proj", [DEPTH, PLE, D])):
        wd[name] = dr(name, shape, F32, "ExternalInput")
    out_d = dr("outT", [D, S], F32, "ExternalOutput")
    XA = dr("XA", [D, S], F32, "Internal")
    XB = dr("XB", [D, S], F32, "Internal")
    HH = dr("HH", [FF, S], BF16, "Internal")
    KT = dr("KT", [D, S], BF16, "Internal")
    VV = dr("VV", [S, D], BF16, "Internal")
    QT = dr("QT", [D, S], BF16, "Internal")
    OT = dr("OT", [D, S], BF16, "Internal")

    P = Prog(nc)
    sb = lambda name, shape, dt: nc.alloc_sbuf_tensor("sb_" + name, shape, dt).ap()
    vecs = sb("vecs", [128, NV], F32)
    cst = sb("consts", [128, C_MASK], F32)
    ones_r = sb("ones_r", [128, 128], F32)
    idb = sb("idb", [128, 128], BF16)
    tri16 = sb("tri16", [128, 128], F16)
    ones16 = sb("ones16", [128, 128], F16)
    yT_g = sb("yT", [128, KC, TB], F32)
    sqrot_g = [sb(f"sqrot{i}", [128, TB], F32) for i in range(2)]
    ARENA_BYTES = nc.sbuf_bytes_remaining - 1024
    arena = Arena(sb("arena", [128, ARENA_BYTES], U8), ARENA_BYTES)
    ps = [nc.alloc_psum_tensor(f"ps{i}", [128, 512], F32).ap() for i in range(8)]

    P.dma("sp", vecs, vecs_d, "vecs", writes=["vecs"])
    P.dma("sp", cst, consts_d[:, 0:C_MASK], "consts", writes=["consts"])
    P.op("dve", lambda e: e.tensor_copy(out=ones_r.bitcast(F32R), in_=cst[:, C_ONES:C_ONES + 128]), reads=["consts"], writes=["k"])
    P.op("dve", lambda e: e.tensor_copy(out=idb, in_=cst[:, C_ID:C_ID + 128]), reads=["consts"], writes=["k1"])
    P.op("dve", lambda e: e.tensor_copy(out=tri16, in_=cst[:, C_TRI:C_TRI + 128]), reads=["consts"], writes=["k2"])
    P.op("dve", lambda e: e.tensor_copy(out=ones16, in_=cst[:, C_ONES:C_ONES + 128]), reads=["consts"], writes=["k3"])
    P.barrier()
    zeros32 = cst[:, C_ZERO:C_ZERO + 512]

    def vcol(key, j):
        c = VOFF[key] + j
        return vecs[:, c:c + 1]

    cast_rr = [0]
    CAST_ENG = ("act", "dve", "act", "dve", "act", "pool")

    def load_weight(dst, src, K, N, stg):
        CH = stg[0].shape[1]
        ns = len(stg)
        for kc in range(K // 128):
            for c0 in range(0, N, CH):
                cw = min(CH, N - c0)
                i = cast_rr[0] % ns
                eng = CAST_ENG[cast_rr[0] % len(CAST_ENG)]
                cast_rr[0] += 1
                P.dma("sp", stg[i][:, 0:cw], src[kc * 128:(kc + 1) * 128, c0:c0 + cw], f"stg{i}", writes=[f"stg{i}"])
                if eng == "act":
                    P.op("act", lambda e, o=dst[:, kc, c0:c0 + cw], s=stg[i][:, 0:cw]: e.activation(out=o, in_=s, func=AF.Copy),
                         reads=[f"stg{i}"], writes=[f"w{cast_rr[0]}"])
                else:
                    P.op(eng, lambda e, o=dst[:, kc, c0:c0 + cw], s=stg[i][:, 0:cw]: e.tensor_copy(out=o, in_=s),
                         reads=[f"stg{i}"], writes=[f"w{cast_rr[0]}"])

    def blk(ap2d, b):
        return ap2d.rearrange("(c p) t -> p c t", p=128)[:, :, b * TB:(b + 1) * TB]

    def load_xb(Xin, b, xrot, xb, tag):
        for j in range(KC):
            i = j % 2
            P.dma("pool", xrot[i], Xin[j * 128:(j + 1) * 128, b * TB:(b + 1) * TB], f"xrot{i}", writes=[f"xrot{i}"])
            P.op("pool", lambda e, o=xb[:, j, :], s=xrot[i]: e.tensor_copy(out=o, in_=s), reads=[f"xrot{i}"], writes=[f"{tag}{j}"])

    def stats_finish(s1, s2, mu, rstd, nmr):
        P.op("dve", lambda e: e.tensor_scalar(out=mu, in0=ps[s1], scalar1=1.0 / D, scalar2=None, op0=ALU.mult), reads=[f"ps{s1}"], writes=["mu"])
        P.op("dve", lambda e: e.tensor_tensor(out=nmr, in0=mu, in1=mu, op=ALU.mult), reads=["mu"], writes=["nmr"])
        P.op("dve", lambda e: e.scalar_tensor_tensor(out=rstd, in0=ps[s2], scalar=1.0 / D, in1=nmr, op0=ALU.mult, op1=ALU.subtract),
             reads=[f"ps{s2}", "nmr"], writes=["rstd"])
        P.op("act", lambda e: e.activation(out=rstd, in_=rstd, func=AF.Sqrt, bias=EPS), reads=["rstd"], writes=["rstd"])
        P.op("dve", lambda e: e.reciprocal(out=rstd, in_=rstd), reads=["rstd"], writes=["rstd"])
        P.op("dve", lambda e: e.scalar_tensor_tensor(out=nmr, in0=mu, scalar=-1.0, in1=rstd, op0=ALU.mult, op1=ALU.mult),
             reads=["mu", "rstd"], writes=["nmr"])

    def stat_sq(j, ysrc, yname, sqrot):
        i = j % 2
        P.op("act", lambda e, o=sqrot[i], s=ysrc: e.activation(out=o.bitcast(F32R), in_=s, func=AF.Square), reads=[yname], writes=[f"sq{i}"])

    def stat_mm(j, ysrc, yname, sqrot, last, s1=6, s2=7):
        i = j % 2
        P.op("pe", lambda e, s=ysrc: e.matmul(ps[s1], lhsT=ones_r.bitcast(F32R), rhs=s.bitcast(F32R), start=(j == 0), stop=last),
             reads=[yname], writes=[f"ps{s1}"], signal=last)
        P.op("pe", lambda e, s=sqrot[i]: e.matmul(ps[s2], lhsT=ones_r.bitcast(F32R), rhs=s.bitcast(F32R), start=(j == 0), stop=last),
             reads=[f"sq{i}"], writes=[f"ps{s2}"], signal=True)

    class Defer:
        def __init__(self):
            self.f = None

        def flush(self):
            if self.f is not None:
                self.f()
                self.f = None

    def ln_out(yT, rstd, nmr, trot, orot, gkey, bkey, Xout, b):
        for j in range(KC):
            i = j % len(trot)
            P.op("dve", lambda e, o=trot[i], s=yT[:, j, :]: e.tensor_tensor(out=o, in0=s, in1=rstd, op=ALU.mult),
                 reads=[f"y{j}", "rstd"], writes=[f"trot{i}"])
            P.op("dve", lambda e, o=trot[i]: e.tensor_tensor(out=o, in0=o, in1=nmr, op=ALU.add),
                 reads=[f"trot{i}", "nmr"], writes=[f"trot{i}"])
            P.op("act", lambda e, o=orot[i], s=trot[i], j=j: e.activation(out=o, in_=s, func=AF.Identity, bias=vcol(bkey, j), scale=vcol(gkey, j)),
                 reads=[f"trot{i}"], writes=[f"orot{i}"])
            P.dma("pool", Xout[j * 128:(j + 1) * 128, b * TB:(b + 1) * TB], orot[i], f"orot{i}", reads=[f"orot{i}"])

    def ln_bufs():
        trot = [arena.take([128, TB], F32, 4) for _ in range(4)]
        orot = [arena.take([128, TB], F32, 4) for _ in range(4)]
        mu = arena.take([128, TB], F32, 4)
        rstd = arena.take([128, TB], F32, 4)
        nmr = arena.take([128, TB], F32, 4)
        return yT_g, sqrot_g, trot, orot, mu, rstd, nmr

    def phase_C(l, Xin, Xout):
        arena.reset()
        W1 = arena.take([128, KC, 2 * D], BF16, 2)
        W2 = arena.take([128, KC, D], BF16, 2)
        DG = arena.take([128, CW * 8, 128], BF16, 2)
        stg = [arena.take([128, 256], F32, 4) for _ in range(2)]
        xrot = [arena.take([128, TB], F32, 4) for _ in range(4)]
        xbs = [arena.take([128, KC, TB], BF16, 2) for _ in range(2)]
        hT = arena.take([128, KC, 30 + TB], BF16, 2)
        yT, sqrot, trot, orot, mu, rstd, nmr = ln_bufs()
        sg = trot[2:4]
        load_weight(W1, wd["a_pw1_w"][l], D, 2 * D, stg)
        load_weight(W2, wd["a_pw2_w"][l], D, D, stg)
        for k in range(CW):
            for j in range(8):
                idx = k * 8 + j
                if idx % 2:
                    P.op("dve", lambda e, o=DG[:, idx, :], idx=idx: e.tensor_scalar(out=o, in0=idb, scalar1=vcol(("dww", l), idx), scalar2=None, op0=ALU.mult),
                         writes=[f"dg{idx}"])
                else:
                    P.op("act", lambda e, o=DG[:, idx, :], idx=idx: e.activation(out=o, in_=idb, func=AF.Identity, scale=vcol(("dww", l), idx)),
                         writes=[f"dg{idx}"])
        P.barrier()
        load_xb(Xin, 0, xrot, xbs[0], "xb0_")
        for b in range(NBLK):
            xb = xbs[b % 2]
            tag = f"xb{b % 2}_"
            if b + 1 < NBLK:
                load_xb(Xin, b + 1, xrot, xbs[(b + 1) % 2], f"xb{(b + 1) % 2}_")
            if b == 0:
                P.op("dve", lambda e: e.tensor_copy(out=hT[:, :, 0:30], in_=zeros32[:, 0:240].rearrange("p (a b) -> p a b", a=8)),
                     writes=[f"h{j}" for j in range(8)])
            else:
                P.op("dve", lambda e: e.tensor_copy(out=hT[:, :, 0:30], in_=hT[:, :, TB:TB + 30]),
                     reads=[f"h{j}" for j in range(8)], writes=[f"h{j}" for j in range(8)])
            for j in range(8):
                pa, pg = j % 2, 2 + j % 2
                for k in range(KC):
                    P.op("pe", lambda e, k=k, j=j, pa=pa, xb=xb: e.matmul(ps[pa], lhsT=W1[:, k, j * 128:(j + 1) * 128], rhs=xb[:, k, :], start=(k == 0), stop=(k == KC - 1)),
                         reads=[f"{tag}{k}"], writes=[f"ps{pa}"], signal=(k == KC - 1))
                for k in range(KC):
                    P.op("pe", lambda e, k=k, j=j, pg=pg, xb=xb: e.matmul(ps[pg], lhsT=W1[:, k, D + j * 128:D + (j + 1) * 128], rhs=xb[:, k, :], start=(k == 0), stop=(k == KC - 1)),
                         reads=[f"{tag}{k}"], writes=[f"ps{pg}"], signal=(k == KC - 1))
                P.op("act", lambda e, j=j, pg=pg: e.activation(out=sg[j % 2], in_=ps[pg], func=AF.Sigmoid, bias=vcol(("pw1b", l), 8 + j)),
                     reads=[f"ps{pg}"], writes=[f"trot{2 + j % 2}"])
                P.op("dve", lambda e, j=j, pa=pa: e.scalar_tensor_tensor(out=hT[:, j, 30:30 + TB], in0=ps[pa], scalar=vcol(("pw1b", l), j), in1=sg[j % 2], op0=ALU.add, op1=ALU.mult),
                     reads=[f"ps{pa}", f"trot{2 + j % 2}"], writes=[f"h{j}"])
            df = Defer()
            for j in range(8):
                pc = 4 + j % 2
                for k in range(CW):
                    P.op("pe", lambda e, k=k, j=j, pc=pc: e.matmul(ps[pc], lhsT=DG[:, k * 8 + j, :], rhs=hT[:, j, k:k + TB], start=(k == 0), stop=(k == CW - 1)),
                         reads=[f"h{j}", f"dg{k * 8 + j}"], writes=[f"ps{pc}"], signal=(k == CW - 1))
                df.flush()
                P.op("act", lambda e, j=j, pc=pc: e.activation(out=yT[:, j, :].bitcast(F32R), in_=ps[pc], func=AF.Identity, bias=vcol(("dwb", l), j)),
                     reads=[f"ps{pc}"], writes=[f"y{j}"])
                stat_sq(j, yT[:, j, :], f"y{j}", sqrot)
                df.f = (lambda j=j: stat_mm(j, yT[:, j, :], f"y{j}", sqrot, j == 7))
            df.flush()
            stats_finish(6, 7, mu, rstd, nmr)
            for j in range(8):
                i = j % 2
                P.op("dve", lambda e, o=trot[i], s=yT[:, j, :]: e.tensor_tensor(out=o, in0=s, in1=rstd, op=ALU.mult),
                     reads=[f"y{j}", "rstd"], writes=[f"trot{i}"])
                P.op("dve", lambda e, o=trot[i]: e.tensor_tensor(out=o, in0=o, in1=nmr, op=ALU.add),
                     reads=[f"trot{i}", "nmr"], writes=[f"trot{i}"])
                P.op("act", lambda e, o=xb[:, j, :], s=trot[i], j=j: e.activation(out=o, in_=s, func=AF.Silu, bias=vcol(("lnb", l), j), scale=vcol(("lng", l), j)),
                     reads=[f"trot{i}"], writes=[f"{tag}{j}"])
            for j in range(8):
                pm = 4 + j % 2
                i = j % 2
                for k in range(KC):
                    P.op("pe", lambda e, k=k, j=j, pm=pm, xb=xb: e.matmul(ps[pm], lhsT=W2[:, k, j * 128:(j + 1) * 128], rhs=xb[:, k, :], start=(k == 0), stop=(k == KC - 1)),
                         reads=[f"{tag}{k}"], writes=[f"ps{pm}"], signal=(k == KC - 1))
                df.flush()
                P.op("act", lambda e, j=j, pm=pm, i=i: e.activation(out=sg[i], in_=ps[pm], func=AF.Identity, bias=vcol(("pw2b", l), j)),
                     reads=[f"ps{pm}"], writes=[f"trot{2 + i}"])
                P.dma("sp", xrot[2 + i], Xin[j * 128:(j + 1) * 128, b * TB:(b + 1) * TB], f"xrot{2 + i}", writes=[f"xrot{2 + i}"])
                P.op("dve", lambda e, j=j, i=i: e.scalar_tensor_tensor(out=yT[:, j, :].bitcast(F32R), in0=xrot[2 + i], scalar=ALPHA, in1=sg[i], op0=ALU.mult, op1=ALU.add),
                     reads=[f"xrot{2 + i}", f"trot{2 + i}"], writes=[f"y{j}"])
                stat_sq(j, yT[:, j, :], f"y{j}", sqrot)
                df.f = (lambda j=j: stat_mm(j, yT[:, j, :], f"y{j}", sqrot, j == 7))
            df.flush()
            stats_finish(6, 7, mu, rstd, nmr)
            ln_out(yT, rstd, nmr, trot, orot, ("mixg", l), ("mixb", l), Xout, b)
        P.barrier()

    def phase_F1(l, Xin):
        arena.reset()
        Wu = arena.take([128, KC, FF], BF16, 2)
        Wg = arena.take([128, KC, FF], BF16, 2)
        stg = [arena.take([128, 1024], F32, 4) for _ in range(4)]
        xrot = [arena.take([128, TB], F32, 4) for _ in range(2)]
        xbs = [arena.take([128, KC, TB], BF16, 2) for _ in range(2)]
        gs = [arena.take([128, TB + 2], F32, 4) for _ in range(3)]
        acc = [arena.take([128, TB], F32, 4) for _ in range(3)]
        hrot = [arena.take([128, TB], BF16, 2) for _ in range(3)]
        gprev = arena.take([128, FC, 2], F32, 4)
        load_weight(Wu, wd["ffn_w_up"][l], D, FF, stg)
        load_weight(Wg, wd["ffn_w_gate"][l], D, FF, stg)
        P.op("dve", lambda e: e.tensor_copy(out=gprev.rearrange("p a b -> p (a b)"), in_=zeros32[:, 0:2 * FC]), writes=["gprev"])
        P.barrier()
        load_xb(Xin, 0, xrot, xbs[0], "xb0_")
        cnt = 0
        for b in range(NBLK):
            xb = xbs[b % 2]
            tag = f"xb{b % 2}_"
            if b + 1 < NBLK:
                load_xb(Xin, b + 1, xrot, xbs[(b + 1) % 2], f"xb{(b + 1) % 2}_")
            for f in range(FC):
                i = cnt % 3
                cnt += 1
                pu, pg = i, 3 + i
                for k in range(KC):
                    P.op("pe", lambda e, k=k, f=f, pu=pu, xb=xb: e.matmul(ps[pu], lhsT=Wu[:, k, f * 128:(f + 1) * 128], rhs=xb[:, k, :], start=(k == 0), stop=(k == KC - 1)),
                         reads=[f"{tag}{k}"], writes=[f"ps{pu}"], signal=(k == KC - 1))
                for k in range(KC):
                    P.op("pe", lambda e, k=k, f=f, pg=pg, xb=xb: e.matmul(ps[pg], lhsT=Wg[:, k, f * 128:(f + 1) * 128], rhs=xb[:, k, :], start=(k == 0), stop=(k == KC - 1)),
                         reads=[f"{tag}{k}"], writes=[f"ps{pg}"], signal=(k == KC - 1))
                P.op("act", lambda e, i=i, pg=pg: e.activation(out=gs[i][:, 2:2 + TB], in_=ps[pg], func=AF.Copy), reads=[f"ps{pg}"], writes=[f"gs{i}"])
                P.op("pool", lambda e, i=i, f=f: e.tensor_copy(out=gs[i][:, 0:2], in_=gprev[:, f, :]), reads=["gprev"], writes=[f"gs{i}"])
                P.op("act", lambda e, i=i, pg=pg, f=f: e.activation(out=acc[i], in_=ps[pg], func=AF.Identity, bias=vcol(("fcb", l), f), scale=vcol(("fcw", l), 2 * FC + f)),
                     reads=[f"ps{pg}"], writes=[f"acc{i}"])
                P.op("dve", lambda e, i=i, f=f: e.scalar_tensor_tensor(out=acc[i], in0=gs[i][:, 1:1 + TB], scalar=vcol(("fcw", l), FC + f), in1=acc[i], op0=ALU.mult, op1=ALU.add),
                     reads=[f"gs{i}", f"acc{i}"], writes=[f"acc{i}"])
                P.op("dve", lambda e, i=i, f=f: e.scalar_tensor_tensor(out=acc[i], in0=gs[i][:, 0:TB], scalar=vcol(("fcw", l), f), in1=acc[i], op0=ALU.mult, op1=ALU.add),
                     reads=[f"gs{i}", f"acc{i}"], writes=[f"acc{i}"])
                P.op("pool", lambda e, i=i, f=f: e.tensor_copy(out=gprev[:, f, :], in_=gs[i][:, TB:TB + 2]), reads=[f"gs{i}"], writes=["gprev"])
                P.op("act", lambda e, i=i: e.activation(out=acc[i], in_=acc[i], func=AF.Silu), reads=[f"acc{i}"], writes=[f"acc{i}"])
                P.op("dve", lambda e, i=i, pu=pu: e.tensor_tensor(out=hrot[i], in0=acc[i], in1=ps[pu], op=ALU.mult),
                     reads=[f"acc{i}", f"ps{pu}"], writes=[f"hrot{i}"])
                P.dma("pool", HH[f * 128:(f + 1) * 128, b * TB:(b + 1) * TB], hrot[i], f"hrot{i}", reads=[f"hrot{i}"])
        P.barrier()

    def phase_F2(l, Xin, Xout):
        arena.reset()
        Wd = arena.take([128, FC, D], BF16, 2)
        Wq = arena.take([128, KC, D], BF16, 2)
        Wp = arena.take([128, 2, D], BF16, 2)
        stg = [arena.take([128, 512], F32, 4) for _ in range(2)]
        xrot = [arena.take([128, TB], F32, 4) for _ in range(4)]
        xbs = [arena.take([128, KC, TB], BF16, 2) for _ in range(2)]
        hhs = [arena.take([128, FC, TB], BF16, 2) for _ in range(2)]
        p32 = arena.take([128, 2, TB], F32, 4)
        pbs = [arena.take([128, 2, TB], BF16, 2) for _ in range(2)]
        yT, sqrot, trot, orot, mu, rstd, nmr = ln_bufs()
        sgq = trot[0:2]
        load_weight(Wd, wd["ffn_w_down"][l], FF, D, stg)
        load_weight(Wq, wd["ple_w_gate"][l], D, D, stg)
        load_weight(Wp, wd["ple_w_proj"][l], PLE, D, stg)
        P.barrier()

        def load_in(b):
            par = b % 2
            load_xb(Xin, b, xrot, xbs[par], f"xb{par}_")
            P.dma("pool", hhs[par][:, 0:11, :], blk(HH, b)[:, 0:11, :], f"hh{par}_0", writes=[f"hh{par}_0"])
            P.dma("pool", hhs[par][:, 11:22, :], blk(HH, b)[:, 11:22, :], f"hh{par}_1", writes=[f"hh{par}_1"])
            P.dma("pool", p32, blk(pT_d[l], b), "p32", writes=["p32"])
            P.op("pool", lambda e, o=pbs[par]: e.tensor_copy(out=o, in_=p32), reads=["p32"], writes=[f"pb{par}"])

        load_in(0)
        for b in range(NBLK):
            par = b % 2
            xb, hh, pb = xbs[par], hhs[par], pbs[par]
            tag = f"xb{par}_"
            if b + 1 < NBLK:
                load_in(b + 1)
            df = Defer()
            for j in range(8):
                i = j % 2
                pf, pq, pr = i, 2 + i, 4 + i
                for f in range(FC):
                    P.op("pe", lambda e, f=f, j=j, pf=pf, hh=hh: e.matmul(ps[pf], lhsT=Wd[:, f, j * 128:(j + 1) * 128], rhs=hh[:, f, :], start=(f == 0), stop=(f == FC - 1)),
                         reads=[f"hh{par}_{f // 11}"], writes=[f"ps{pf}"], signal=(f == FC - 1))
                for k in range(KC):
                    P.op("pe", lambda e, k=k, j=j, pq=pq, xb=xb: e.matmul(ps[pq], lhsT=Wq[:, k, j * 128:(j + 1) * 128], rhs=xb[:, k, :], start=(k == 0), stop=(k == KC - 1)),
                         reads=[f"{tag}{k}"], writes=[f"ps{pq}"], signal=(k == KC - 1))
                for c in range(2):
                    P.op("pe", lambda e, c=c, j=j, pr=pr, pb=pb: e.matmul(ps[pr], lhsT=Wp[:, c, j * 128:(j + 1) * 128], rhs=pb[:, c, :], start=(c == 0), stop=(c == 1)),
                         reads=[f"pb{par}"], writes=[f"ps{pr}"], signal=(c == 1))
                df.flush()
                P.op("act", lambda e, i=i, pq=pq: e.activation(out=sgq[i], in_=ps[pq], func=AF.Sigmoid), reads=[f"ps{pq}"], writes=[f"trot{i}"])
                P.op("dve", lambda e, i=i, pr=pr: e.tensor_tensor(out=sgq[i], in0=sgq[i], in1=ps[pr], op=ALU.mult), reads=[f"trot{i}", f"ps{pr}"], writes=[f"trot{i}"])
                P.op("dve", lambda e, i=i, pf=pf: e.tensor_tensor(out=sgq[i], in0=sgq[i], in1=ps[pf], op=ALU.add), reads=[f"trot{i}", f"ps{pf}"], writes=[f"trot{i}"])
                P.dma("sp", xrot[2 + i], Xin[j * 128:(j + 1) * 128, b * TB:(b + 1) * TB], f"xrot{2 + i}", writes=[f"xrot{2 + i}"])
                P.op("dve", lambda e, i=i, j=j: e.scalar_tensor_tensor(out=yT[:, j, :].bitcast(F32R), in0=xrot[2 + i], scalar=ALPHA, in1=sgq[i], op0=ALU.mult, op1=ALU.add),
                     reads=[f"xrot{2 + i}", f"trot{i}"], writes=[f"y{j}"])
                stat_sq(j, yT[:, j, :], f"y{j}", sqrot)
                df.f = (lambda j=j: stat_mm(j, yT[:, j, :], f"y{j}", sqrot, j == 7))
            df.flush()
            stats_finish(6, 7, mu, rstd, nmr)
            ln_out(yT, rstd, nmr, trot, orot, ("ffg", l), ("ffb", l), Xout, b)
        P.barrier()

    def phase_KVQ(Xin, lq, do_kv):
        arena.reset()
        Wq = arena.take([128, KC, D], BF16, 2)
        if do_kv:
            Wk = arena.take([128, KC, D], BF16, 2)
            Wv = arena.take([128, KC, D], BF16, 2)
        stg = [arena.take([128, 1024], F32, 4) for _ in range(4)]
        xrot = [arena.take([128, TB], F32, 4) for _ in range(2)]
        xbs = [arena.take([128, KC, TB], BF16, 2) for _ in range(2)]
        krot = [arena.take([128, TB], BF16, 2) for _ in range(3)]
        load_weight(Wq, wd["b_wq"][lq], D, D, stg)
        if do_kv:
            load_weight(Wk, wd["kv_wk"], D, D, stg)
            load_weight(Wv, wd["kv_wv"], D, D, stg)
        P.barrier()
        cnt = 0
        load_xb(Xin, 0, xrot, xbs[0], "xb0_")
        for b in range(NBLK):
            xb = xbs[b % 2]
            tag = f"xb{b % 2}_"
            if b + 1 < NBLK:
                load_xb(Xin, b + 1, xrot, xbs[(b + 1) % 2], f"xb{(b + 1) % 2}_")
            for W, dst, scale in ([(Wq, QT, 0.125)] + ([(Wk, KT, 1.0)] if do_kv else [])):
                for j in range(8):
                    i = cnt % 3
                    cnt += 1
                    for k in range(KC):
                        P.op("pe", lambda e, k=k, j=j, i=i, W=W, xb=xb: e.matmul(ps[i], lhsT=W[:, k, j * 128:(j + 1) * 128], rhs=xb[:, k, :], start=(k == 0), stop=(k == KC - 1)),
                             reads=[f"{tag}{k}"], writes=[f"ps{i}"], signal=(k == KC - 1))
                    P.op("act", lambda e, i=i, scale=scale: e.activation(out=krot[i], in_=ps[i], func=AF.Copy, scale=scale), reads=[f"ps{i}"], writes=[f"krot{i}"])
                    P.dma("pool", dst[j * 128:(j + 1) * 128, b * TB:(b + 1) * TB], krot[i], f"krot{i}", reads=[f"krot{i}"])
            if do_kv:
                for t in range(4):
                    for hf in range(2):
                        i = cnt % 3
                        cnt += 1
                        for k in range(KC):
                            P.op("pe", lambda e, k=k, t=t, hf=hf, i=i, xb=xb: e.matmul(ps[i], lhsT=xb[:, k, t * 128:(t + 1) * 128], rhs=Wv[:, k, hf * 512:(hf + 1) * 512], start=(k == 0), stop=(k == KC - 1)),
                                 reads=[f"{tag}{k}"], writes=[f"ps{i}"], signal=(k == KC - 1))
                        P.op("dve", lambda e, i=i: e.tensor_copy(out=krot[i], in_=ps[i]), reads=[f"ps{i}"], writes=[f"krot{i}"])
                        P.dma("pool", VV[b * TB + t * 128:b * TB + (t + 1) * 128, hf * 512:(hf + 1) * 512], krot[i], f"krot{i}", reads=[f"krot{i}"])
        P.barrier()

    def phase_ATT():
        arena.reset()
        Kh = [arena.take([128, S], BF16, 2) for _ in range(2)]
        Qh = [arena.take([128, S], BF16, 2) for _ in range(2)]
        Vh = [arena.take([128, S // 128, 128], BF16, 2) for _ in range(2)]
        for t in (Kh, Qh):
            for i in range(2):
                for c in range(S // 512):
                    P.op("dve", lambda e, o=t[i][64:128, c * 512:(c + 1) * 512]: e.tensor_copy(out=o, in_=zeros32[64:128, :]), writes=["zpad"])
        for i in range(2):
            for c in range(4):
                P.op("pool", lambda e, o=Vh[i][:, c * 8:(c + 1) * 8, 64:128]: e.tensor_copy(out=o, in_=zeros32.rearrange("p (a b) -> p a b", a=8)), writes=["zpad2"])
        mstage = arena.take([128, 2048], F32, 4)
        mask16 = arena.take([128, 4, 512], F16, 2)
        maskb = arena.take([128, 4, 512], BF16, 2)
        P.dma("sp", mstage, consts_d[:, C_MASK:C_MASK + 2048], "mstage", writes=["mstage"])
        P.op("dve", lambda e: e.tensor_copy(out=mask16.rearrange("p a b -> p (a b)"), in_=mstage), reads=["mstage"], writes=["k4"])
        P.op("act", lambda e: e.activation(out=maskb.rearrange("p a b -> p (a b)"), in_=mstage, func=AF.Copy), reads=["mstage"], writes=["k5"])
        P.barrier()
        e32 = [arena.take([128, TB], F32, 4) for _ in range(3)]
        ec32 = [arena.take([128, TB], F32, 4) for _ in range(2)]
        sp16 = [arena.take([128, TB], F16, 2) for _ in range(3)]
        S16 = [arena.take([128, TB], F16, 2) for _ in range(2)]
        a16 = [arena.take([128, TB], BF16, 2) for _ in range(3)]
        orot = [arena.take([64, TB], BF16, 2) for _ in range(2)]
        items = []
        for h in range(NH):
            for qb in range(NBLK):
                kts = list(range(qb * 4 + 3, -1, -1))
                for n, kt in enumerate(kts):
                    items.append((h, qb, kt, n == 0, n == len(kts) - 1, kt - qb * 4))
        NI = len(items)

        def load_head(h):
            hi = h % 2
            P.dma("sp", Kh[hi][0:DH, :], KT[h * DH:(h + 1) * DH, :], f"Kh{hi}", writes=[f"Kh{hi}"])
            P.dma("sp", Qh[hi][0:DH, :], QT[h * DH:(h + 1) * DH, :], f"Qh{hi}", writes=[f"Qh{hi}"])
            for c in range(4):
                P.dma("sp", Vh[hi][:, c * 8:(c + 1) * 8, 0:DH], VV.rearrange("(t p) d -> p t d", p=128)[:, c * 8:(c + 1) * 8, h * DH:(h + 1) * DH],
                      f"Vh{hi}_{c}", writes=[f"Vh{hi}_{c}"])

        st = {"si": 0}

        def stage_A(t):
            h, qb, kt, first, last, d = items[t]
            hi, i, pz = h % 2, t % 3, t % 3
            P.op("pe", lambda e: e.matmul(ps[pz], lhsT=Kh[hi][:, kt * 128:(kt + 1) * 128], rhs=Qh[hi][:, qb * TB:(qb + 1) * TB], start=True, stop=True),
                 reads=[f"Kh{hi}", f"Qh{hi}"], writes=[f"ps{pz}"], signal=True)
            P.op("act", lambda e: e.activation(out=e32[i], in_=ps[pz], func=AF.Exp), reads=[f"ps{pz}"], writes=[f"e{i}"])
            P.op("act", lambda e: e.activation(out=sp16[i], in_=e32[i], func=AF.Ln, bias=1.0), reads=[f"e{i}"], writes=[f"sp{i}"])
            if d >= 0:
                P.op("pool", lambda e: e.tensor_tensor(out=sp16[i], in0=sp16[i], in1=mask16[:, d, :], op=ALU.mult), reads=[f"sp{i}"], writes=[f"sp{i}"])

        def stage_B(t):
            h, qb, kt, first, last, d = items[t]
            i, pc, j = t % 3, 3 + t % 3, t % 2
            si = st["si"]
            P.op("pe", lambda e: e.matmul(ps[pc], lhsT=tri16, rhs=sp16[i], start=True, stop=first),
                 reads=[f"sp{i}"], writes=[f"ps{pc}"], signal=first)
            if not first:
                P.op("pe", lambda e: e.matmul(ps[pc], lhsT=ones16, rhs=S16[si], start=False, stop=True),
                     reads=[f"S{si}"], writes=[f"ps{pc}"], signal=True)
            if not last:
                if first:
                    P.op("dve", lambda e: e.tensor_copy(out=S16[0], in_=sp16[i]), reads=[f"sp{i}"], writes=["S0"])
                    st["si"] = 0
                else:
                    P.op("dve", lambda e: e.tensor_tensor(out=S16[1 - si], in0=S16[si], in1=sp16[i], op=ALU.add),
                         reads=[f"sp{i}", f"S{si}"], writes=[f"S{1 - si}"])
                    st["si"] = 1 - si
            P.op("act", lambda e: e.activation(out=ec32[j], in_=ps[pc], func=AF.Exp, scale=-1.0), reads=[f"ps{pc}"], writes=[f"ec{j}"])
            P.op("dve", lambda e: e.tensor_tensor(out=a16[i], in0=e32[i], in1=ec32[j], op=ALU.mult), reads=[f"e{i}", f"ec{j}"], writes=[f"a{i}"])
            if d >= 0:
                P.op("pool", lambda e: e.tensor_tensor(out=a16[i], in0=a16[i], in1=maskb[:, d, :], op=ALU.mult), reads=[f"a{i}"], writes=[f"a{i}"])

        def stage_C(t):
            h, qb, kt, first, last, d = items[t]
            hi, i = h % 2, t % 3
            hq = h * NBLK + qb
            po = 6 + hq % 2
            P.op("pe", lambda e: e.matmul(ps[po], lhsT=Vh[hi][:, kt, :], rhs=a16[i], start=first, stop=last),
                 reads=[f"a{i}", f"Vh{hi}_{kt // 8}"], writes=[f"ps{po}"], signal=last)
            if last:
                oi = hq % 2
                P.op("dve", lambda e: e.tensor_copy(out=orot[oi], in_=ps[po][0:DH, :]), reads=[f"ps{po}"], writes=[f"orot{oi}"])
                P.dma("pool", OT[h * DH:(h + 1) * DH, qb * TB:(qb + 1) * TB], orot[oi], f"orot{oi}", reads=[f"orot{oi}"])
                if qb == NBLK - 1 and h + 2 < NH:
                    load_head(h + 2)

        load_head(0)
        load_head(1)
        for t in range(NI + 2):
            if t < NI:
                stage_A(t)
            if 0 <= t - 1 < NI:
                stage_B(t - 1)
            if 0 <= t - 2 < NI:
                stage_C(t - 2)
        P.barrier()

    def phase_WO(l, Xin, Xout):
        arena.reset()
        Wo = arena.take([128, KC, D], BF16, 2)
        stg = [arena.take([128, 1024], F32, 4) for _ in range(4)]
        xrot = [arena.take([128, TB], F32, 4) for _ in range(4)]
        xbs = [arena.take([128, KC, TB], BF16, 2) for _ in range(2)]
        yT, sqrot, trot, orot, mu, rstd, nmr = ln_bufs()
        load_weight(Wo, wd["b_wo"][l - NA], D, D, stg)
        P.barrier()
        P.dma("pool", xbs[0], blk(OT, 0), "xbfull0", writes=[f"xb0_{k}" for k in range(KC)])
        for b in range(NBLK):
            xb = xbs[b % 2]
            tag = f"xb{b % 2}_"
            if b + 1 < NBLK:
                P.dma("pool", xbs[(b + 1) % 2], blk(OT, b + 1), f"xbfull{(b + 1) % 2}", writes=[f"xb{(b + 1) % 2}_{k}" for k in range(KC)])
            df = Defer()
            for j in range(8):
                pm = 4 + j % 2
                i = j % 2
                for k in range(KC):
                    P.op("pe", lambda e, k=k, j=j, pm=pm, xb=xb: e.matmul(ps[pm], lhsT=Wo[:, k, j * 128:(j + 1) * 128], rhs=xb[:, k, :], start=(k == 0), stop=(k == KC - 1)),
                         reads=[f"{tag}{k}"], writes=[f"ps{pm}"], signal=(k == KC - 1))
                df.flush()
                P.dma("sp", xrot[2 + i], Xin[j * 128:(j + 1) * 128, b * TB:(b + 1) * TB], f"xrot{2 + i}", writes=[f"xrot{2 + i}"])
                P.op("dve", lambda e, j=j, i=i, pm=pm: e.scalar_tensor_tensor(out=yT[:, j, :].bitcast(F32R), in0=xrot[2 + i], scalar=ALPHA, in1=ps[pm], op0=ALU.mult, op1=ALU.add),
                     reads=[f"xrot{2 + i}", f"ps{pm}"], writes=[f"y{j}"])
                stat_sq(j, yT[:, j, :], f"y{j}", sqrot)
                df.f = (lambda j=j: stat_mm(j, yT[:, j, :], f"y{j}", sqrot, j == 7))
            df.flush()
            stats_finish(6, 7, mu, rstd, nmr)
            ln_out(yT, rstd, nmr, trot, orot, ("mixg", l), ("mixb", l), Xout, b)
        P.barrier()

    cur = xT_d
    for l in range(nlayers):
        last = (l == nlayers - 1)
        if l < NA:
            phase_C(l, cur, XA)
        else:
            phase_KVQ(cur, l - NA, do_kv=(l == NA))
            phase_ATT()
            phase_WO(l, cur, XA)
        phase_F1(l, XA)
        nxt = out_d if last else XB
        phase_F2(l, XA, nxt)
        cur = nxt
    with nc.Block() as block:
        P.replay(block)
    return nc


_NC_CACHE = {}


def kernel(**inputs):
    inp = {k: np.asarray(v) for k, v in inputs.items()}
    if "nc" not in _NC_CACHE:
        _NC_CACHE["nc"] = build(DEPTH)
    nc = _NC_CACHE["nc"]
    vecs = host_vecs(inp)
    consts = host_consts()
    shared = {"vecs": vecs, "consts": consts}
    for name in ("a_pw1_w", "a_pw2_w", "b_wq", "kv_wk", "kv_wv", "b_wo", "ffn_w_up", "ffn_w_gate",
                 "ffn_w_down", "ple_w_gate", "ple_w_proj"):
        shared[name] = np.ascontiguousarray(inp[name], dtype=np.float32)
    in_maps = []
    for c in range(NCORES):
        m = dict(shared)
        m["xT"] = np.ascontiguousarray(inp["x"][c].T)
        m["pT"] = np.ascontiguousarray(np.transpose(inp["p"][:, c], (0, 2, 1)))
        in_maps.append(m)
    res = run_bass_kernel_spmd(nc, in_maps, core_ids=list(range(NCORES)))
    out = np.stack([np.asarray(r["outT"]).T for r in res.results], axis=0)
    return np.ascontiguousarray(out.astype(np.float32))
```

```python
import numpy as np
import concourse.bass as bass
import concourse.mybir as mybir
from concourse.bass_utils import run_bass_kernel_spmd

F32 = mybir.dt.float32
BF16 = mybir.dt.bfloat16
F16 = mybir.dt.float16
F32R = mybir.dt.float32r
U8 = mybir.dt.uint8
AF = mybir.ActivationFunctionType
ALU = mybir.AluOpType

D = 1024
S = 4096
FF = 2816
DEPTH = 4
NA = 2
NH = 16
DH = 64
CW = 31
PLE = 256
TB = 512
NBLK = S // TB
KC = D // 128
FC = FF // 128
ALPHA = float((2.0 * DEPTH) ** 0.25)
EPS = 1e-5
NCORES = 8

ENG = ("pe", "act", "dve", "pool", "sp")


class Tok:
    __slots__ = ("sem", "val", "eng")

    def __init__(self, sem=None, val=None, eng=None):
        self.sem = sem
        self.val = val
        self.eng = eng


class Prog:
    def __init__(self, nc):
        self.nc = nc
        self.thunks = {e: [] for e in ENG}
        self.psem = {e: nc.alloc_semaphore("prog_" + e) for e in ENG}
        self.pcnt = {e: 0 for e in ENG}
        self.W = {}
        self.R = {}
        self.pe_pending = []
        self.dsem = {}

    def _deps(self, reads, writes):
        deps = []
        for r in reads:
            t = self.W.get(r)
            if t is not None:
                deps.append(t)
        for w in writes:
            t = self.W.get(w)
            if t is not None:
                deps.append(t)
            deps.extend(self.R.get(w, ()))
        return deps

    def _commit(self, tok, reads, writes):
        for r in reads:
            self.R.setdefault(r, []).append(tok)
        for w in writes:
            self.W[w] = tok
            self.R[w] = []

    def op(self, eng, fn, reads=(), writes=(), signal=None):
        if signal is None:
            signal = eng != "pe"
        deps = self._deps(reads, writes)
        if eng == "pe":
            deps = [d for d in deps if d.eng != "pe"]
        tok = Tok(eng=eng)
        if signal:
            self.pcnt[eng] += 1
            tok.sem = self.psem[eng]
            tok.val = self.pcnt[eng]
            if eng == "pe":
                for p in self.pe_pending:
                    p.sem = tok.sem
                    p.val = tok.val
                self.pe_pending = []
        else:
            self.pe_pending.append(tok)
        self._commit(tok, reads, writes)
        self.thunks[eng].append((deps, fn, self.psem[eng] if signal else None, 1))
        return tok

    def dma(self, eng, out, in_, slot, reads=(), writes=()):
        ds = self.dsem.get(slot)
        if ds is None:
            ds = [self.nc.alloc_semaphore("d_" + slot), 0]
            self.dsem[slot] = ds
        deps = self._deps(reads, writes)
        if ds[1] > 0:
            deps.append(Tok(ds[0], ds[1]))
        ds[1] += 16
        tok = Tok(ds[0], ds[1])
        self._commit(tok, reads, writes)

        def fn(e, out=out, in_=in_):
            return e.dma_start(out=out, in_=in_)
        self.thunks[eng].append((deps, fn, ds[0], 16))
        return tok

    def barrier(self):
        assert not self.pe_pending
        toks = [Tok(ds[0], ds[1]) for ds in self.dsem.values() if ds[1] > 0]
        toks += [Tok(self.psem[x], self.pcnt[x]) for x in ENG if self.pcnt[x] > 0]
        for e in ENG:
            self.thunks[e].append((list(toks), None, None, 0))
        self.W = {}
        self.R = {}

    def replay(self, block):
        names = {"pe": "tensor", "act": "scalar", "dve": "vector", "pool": "gpsimd", "sp": "sync"}
        for e in ENG:
            thunks = self.thunks[e]
            if not thunks:
                continue

            def body(h, thunks=thunks):
                waited = {}
                for deps, fn, sem, inc in thunks:
                    for d in deps:
                        assert d.sem is not None, "unresolved PE token"
                        k = id(d.sem)
                        if waited.get(k, 0) >= d.val:
                            continue
                        waited[k] = d.val
                        h.wait_ge(d.sem, d.val)
                    if fn is None:
                        continue
                    ins = fn(h)
                    if sem is not None:
                        ins.then_inc(sem, inc)
            getattr(block, names[e])(body)


def vec_layout():
    off = {}
    n = 0

    def add(key, cnt):
        nonlocal n
        off[key] = n
        n += cnt
    for l in range(NA):
        add(("pw1b", l), 16)
        add(("dwb", l), 8)
        add(("lng", l), 8)
        add(("lnb", l), 8)
        add(("pw2b", l), 8)
        add(("dww", l), CW * 8)
    for l in range(DEPTH):
        add(("mixg", l), 8)
        add(("mixb", l), 8)
        add(("fcw", l), 3 * FC)
        add(("fcb", l), FC)
        add(("ffg", l), 8)
        add(("ffb", l), 8)
    return off, n


VOFF, NV = vec_layout()
C_ONES, C_TRI, C_ID, C_ZERO, C_MASK, NCONST = 0, 128, 256, 384, 896, 896 + 2048


def host_consts():
    c = np.zeros((128, NCONST), np.float32)
    c[:, C_ONES:C_ONES + 128] = 1.0
    j = np.arange(128)[:, None]
    s = np.arange(128)[None, :]
    c[:, C_TRI:C_TRI + 128] = (j >= s).astype(np.float32)
    c[:, C_ID:C_ID + 128] = np.eye(128, dtype=np.float32)
    i = np.arange(512)[None, :]
    for d in range(4):
        c[:, C_MASK + d * 512:C_MASK + (d + 1) * 512] = ((i - 128 * d) > j).astype(np.float32)
    return c


def host_vecs(inp):
    v = np.zeros((128, NV), np.float32)

    def put(key, vec):
        vec = np.asarray(vec, np.float32)
        n = vec.shape[0] // 128
        v[:, VOFF[key]:VOFF[key] + n] = vec.reshape(n, 128).T
    for l in range(NA):
        put(("pw1b", l), inp["a_pw1_b"][l])
        put(("dwb", l), inp["a_dw_b"][l])
        put(("lng", l), inp["a_ln_g"][l])
        put(("lnb", l), inp["a_ln_b"][l])
        put(("pw2b", l), inp["a_pw2_b"][l])
        put(("dww", l), np.asarray(inp["a_dw_w"][l]).reshape(-1))
    for l in range(DEPTH):
        put(("mixg", l), inp["ln_mix_g"][l])
        put(("mixb", l), inp["ln_mix_b"][l])
        put(("fcw", l), np.asarray(inp["ffn_conv_w"][l]).reshape(-1))
        put(("fcb", l), inp["ffn_conv_b"][l])
        put(("ffg", l), inp["ln_ffn_g"][l])
        put(("ffb", l), inp["ln_ffn_b"][l])
    return v


class Arena:
    def __init__(self, ap, nbytes):
        self.ap = ap
        self.n = nbytes
        self.off = 0

    def reset(self):
        self.off = 0

    def take(self, shape, dt, esz):
        nb = esz * int(np.prod(shape[1:]))
        n = (nb + 63) // 64 * 64
        assert self.off + n <= self.n, ("arena overflow", self.off + n, self.n)
        v = self.ap[0:shape[0], self.off:self.off + nb].bitcast(dt)
        self.off += n
        if len(shape) == 3:
            v = v.rearrange("p (a b) -> p a b", a=shape[1])
        return v


def build(nlayers=DEPTH):
    nc = bass.Bass("TRN2", target_bir_lowering=False)
    dr = lambda name, shape, dt, kind: nc.dram_tensor(name, shape, dt, kind=kind).ap()
    xT_d = dr("xT", [D, S], F32, "ExternalInput")
    pT_d = dr("pT", [DEPTH, PLE, S], F32, "ExternalInput")
    vecs_d = dr("vecs", [128, NV], F32, "ExternalInput")
    consts_d = dr("consts", [128, NCONST], F32, "ExternalInput")
    wd = {}
    for name, shape in (("a_pw1_w", [NA, D, 2 * D]), ("a_pw2_w", [NA, D, D]), ("b_wq", [2, D, D]),
                        ("kv_wk", [D, D]), ("kv_wv", [D, D]), ("b_wo", [2, D, D]),
                        ("ffn_w_up", [DEPTH, D, FF]), ("ffn_w_gate", [DEPTH, D, FF]),
                        ("ffn_w_down", [DEPTH, FF, D]), ("ple_w_gate", [DEPTH, D, D]),
                        ("ple_w_proj", [DEPTH, PLE, D])):
        wd[name] = dr(name, shape, F32, "ExternalInput")
    out_d = dr("outT", [D, S], F32, "ExternalOutput")
    XA = dr("XA", [D, S], F32, "Internal")
    XB = dr("XB", [D, S], F32, "Internal")
    HH = dr("HH", [FF, S], BF16, "Internal")
    KT = dr("KT", [D, S], BF16, "Internal")
    VV = dr("VV", [S, D], BF16, "Internal")
    QT = dr("QT", [D, S], BF16, "Internal")
    OT = dr("OT", [D, S], BF16, "Internal")

    P = Prog(nc)
    sb = lambda name, shape, dt: nc.alloc_sbuf_tensor("sb_" + name, shape, dt).ap()
    vecs = sb("vecs", [128, NV], F32)
    cst = sb("consts", [128, C_MASK], F32)
    ones_r = sb("ones_r", [128, 128], F32)
    idb = sb("idb", [128, 128], BF16)
    tri16 = sb("tri16", [128, 128], F16)
    ones16 = sb("ones16", [128, 128], F16)
    yT_g = sb("yT", [128, KC, TB], F32)
    sqrot_g = [sb(f"sqrot{i}", [128, TB], F32) for i in range(2)]
    ARENA_BYTES = nc.sbuf_bytes_remaining - 1024
    arena = Arena(sb("arena", [128, ARENA_BYTES], U8), ARENA_BYTES)
    pp = [nc.alloc_psum_tensor(f"pp{i}", [128, 1024], F32).ap() for i in range(4)]
    ps = [pp[i // 2][:, (i % 2) * 512:(i % 2 + 1) * 512] for i in range(8)]

    P.dma("sp", vecs, vecs_d, "vecs", writes=["vecs"])
    P.dma("sp", cst, consts_d[:, 0:C_MASK], "consts", writes=["consts"])
    P.op("dve", lambda e: e.tensor_copy(out=ones_r.bitcast(F32R), in_=cst[:, C_ONES:C_ONES + 128]), reads=["consts"], writes=["k"])
    P.op("dve", lambda e: e.tensor_copy(out=idb, in_=cst[:, C_ID:C_ID + 128]), reads=["consts"], writes=["k1"])
    P.op("dve", lambda e: e.tensor_copy(out=tri16, in_=cst[:, C_TRI:C_TRI + 128]), reads=["consts"], writes=["k2"])
    P.op("dve", lambda e: e.tensor_copy(out=ones16, in_=cst[:, C_ONES:C_ONES + 128]), reads=["consts"], writes=["k3"])
    P.barrier()
    zeros32 = cst[:, C_ZERO:C_ZERO + 512]

    def vcol(key, j):
        c = VOFF[key] + j
        return vecs[:, c:c + 1]

    cast_rr = [0]
    CAST_ENG = ("act", "dve", "act", "dve", "act", "pool")

    def load_weight(dst, src, K, N, stg):
        CH = stg[0].shape[1]
        ns = len(stg)
        for kc in range(K // 128):
            for c0 in range(0, N, CH):
                cw = min(CH, N - c0)
                i = cast_rr[0] % ns
                eng = CAST_ENG[cast_rr[0] % len(CAST_ENG)]
                cast_rr[0] += 1
                P.dma("sp", stg[i][:, 0:cw], src[kc * 128:(kc + 1) * 128, c0:c0 + cw], f"stg{i}", writes=[f"stg{i}"])
                if eng == "act":
                    P.op("act", lambda e, o=dst[:, kc, c0:c0 + cw], s=stg[i][:, 0:cw]: e.activation(out=o, in_=s, func=AF.Copy),
                         reads=[f"stg{i}"], writes=[f"w{cast_rr[0]}"])
                else:
                    P.op(eng, lambda e, o=dst[:, kc, c0:c0 + cw], s=stg[i][:, 0:cw]: e.tensor_copy(out=o, in_=s),
                         reads=[f"stg{i}"], writes=[f"w{cast_rr[0]}"])

    def blk(ap2d, b):
        return ap2d.rearrange("(c p) t -> p c t", p=128)[:, :, b * TB:(b + 1) * TB]

    def xb_dma(Xin, b, xrot, j):
        i = j % 2
        P.dma("pool", xrot[i], Xin[j * 128:(j + 1) * 128, b * TB:(b + 1) * TB], f"xrot{i}", writes=[f"xrot{i}"])

    def xb_cast(xrot, xb, tag, j):
        i = j % 2
        P.op("pool", lambda e, o=xb[:, j, :], s=xrot[i]: e.tensor_copy(out=o, in_=s), reads=[f"xrot{i}"], writes=[f"{tag}{j}"])

    def load_xb(Xin, b, xrot, xb, tag):
        for j in range(KC + 1):
            if j < KC:
                xb_dma(Xin, b, xrot, j)
            if j >= 1:
                xb_cast(xrot, xb, tag, j - 1)

    def stats_finish(s1, s2, mu, rstd, nmr):
        P.op("dve", lambda e: e.tensor_scalar(out=mu, in0=ps[s1], scalar1=1.0 / D, scalar2=None, op0=ALU.mult), reads=[f"ps{s1}"], writes=["mu"])
        P.op("dve", lambda e: e.tensor_tensor(out=nmr, in0=mu, in1=mu, op=ALU.mult), reads=["mu"], writes=["nmr"])
        P.op("dve", lambda e: e.scalar_tensor_tensor(out=rstd, in0=ps[s2], scalar=1.0 / D, in1=nmr, op0=ALU.mult, op1=ALU.subtract),
             reads=[f"ps{s2}", "nmr"], writes=["rstd"])
        P.op("act", lambda e: e.activation(out=rstd, in_=rstd, func=AF.Sqrt, bias=EPS), reads=["rstd"], writes=["rstd"])
        P.op("dve", lambda e: e.reciprocal(out=rstd, in_=rstd), reads=["rstd"], writes=["rstd"])
        P.op("dve", lambda e: e.scalar_tensor_tensor(out=nmr, in0=mu, scalar=-1.0, in1=rstd, op0=ALU.mult, op1=ALU.mult),
             reads=["mu", "rstd"], writes=["nmr"])

    def stat_sq(j, ysrc, yname, sqrot):
        i = j % 2
        P.op("act", lambda e, o=sqrot[i], s=ysrc: e.activation(out=o.bitcast(F32R), in_=s, func=AF.Square), reads=[yname], writes=[f"sq{i}"])

    def stat_mm(j, ysrc, yname, sqrot, last, s1=6, s2=7):
        i = j % 2
        P.op("pe", lambda e, s=ysrc: e.matmul(ps[s1], lhsT=ones_r.bitcast(F32R), rhs=s.bitcast(F32R), start=(j == 0), stop=last),
             reads=[yname], writes=[f"ps{s1}"], signal=last)
        P.op("pe", lambda e, s=sqrot[i]: e.matmul(ps[s2], lhsT=ones_r.bitcast(F32R), rhs=s.bitcast(F32R), start=(j == 0), stop=last),
             reads=[f"sq{i}"], writes=[f"ps{s2}"], signal=True)

    class Defer:
        def __init__(self):
            self.f = None

        def flush(self):
            if self.f is not None:
                self.f()
                self.f = None

    def ln_out(yT, rstd, nmr, trot, orot, gkey, bkey, Xout, b):
        for j in range(KC):
            i = j % len(trot)
            P.op("dve", lambda e, o=trot[i], s=yT[:, j, :]: e.tensor_tensor(out=o, in0=s, in1=rstd, op=ALU.mult),
                 reads=[f"y{j}", "rstd"], writes=[f"trot{i}"])
            P.op("dve", lambda e, o=trot[i]: e.tensor_tensor(out=o, in0=o, in1=nmr, op=ALU.add),
                 reads=[f"trot{i}", "nmr"], writes=[f"trot{i}"])
            P.op("act", lambda e, o=orot[i], s=trot[i], j=j: e.activation(out=o, in_=s, func=AF.Identity, bias=vcol(bkey, j), scale=vcol(gkey, j)),
                 reads=[f"trot{i}"], writes=[f"orot{i}"])
            P.dma("pool", Xout[j * 128:(j + 1) * 128, b * TB:(b + 1) * TB], orot[i], f"orot{i}", reads=[f"orot{i}"])

    def ln_bufs():
        trot = [arena.take([128, TB], F32, 4) for _ in range(4)]
        orot = [arena.take([128, TB], F32, 4) for _ in range(4)]
        mu = arena.take([128, TB], F32, 4)
        rstd = arena.take([128, TB], F32, 4)
        nmr = arena.take([128, TB], F32, 4)
        return yT_g, sqrot_g, trot, orot, mu, rstd, nmr

    def phase_C(l, Xin, Xout):
        arena.reset()
        W1 = arena.take([128, KC, 2 * D], BF16, 2)
        W2 = arena.take([128, KC, D], BF16, 2)
        DG = arena.take([128, CW * 8, 128], BF16, 2)
        stg = [arena.take([128, 256], F32, 4) for _ in range(2)]
        xrot = [arena.take([128, TB], F32, 4) for _ in range(4)]
        xbs = [arena.take([128, KC, TB], BF16, 2) for _ in range(2)]
        hT = arena.take([128, KC, 30 + TB], BF16, 2)
        yT, sqrot, trot, orot, mu, rstd, nmr = ln_bufs()
        sg = trot[2:4]
        load_weight(W1, wd["a_pw1_w"][l], D, 2 * D, stg)
        load_weight(W2, wd["a_pw2_w"][l], D, D, stg)
        for k in range(CW):
            for j in range(8):
                idx = k * 8 + j
                if idx % 2:
                    P.op("dve", lambda e, o=DG[:, idx, :], idx=idx: e.tensor_scalar(out=o, in0=idb, scalar1=vcol(("dww", l), idx), scalar2=None, op0=ALU.mult),
                         writes=[f"dg{idx}"])
                else:
                    P.op("act", lambda e, o=DG[:, idx, :], idx=idx: e.activation(out=o, in_=idb, func=AF.Identity, scale=vcol(("dww", l), idx)),
                         writes=[f"dg{idx}"])
        P.barrier()
        load_xb(Xin, 0, xrot, xbs[0], "xb0_")
        for b in range(NBLK):
            xb = xbs[b % 2]
            tag = f"xb{b % 2}_"
            if b + 1 < NBLK:
                load_xb(Xin, b + 1, xrot, xbs[(b + 1) % 2], f"xb{(b + 1) % 2}_")
            if b == 0:
                P.op("dve", lambda e: e.tensor_copy(out=hT[:, :, 0:30], in_=zeros32[:, 0:240].rearrange("p (a b) -> p a b", a=8)),
                     writes=[f"h{j}" for j in range(8)])
            else:
                P.op("dve", lambda e: e.tensor_copy(out=hT[:, :, 0:30], in_=hT[:, :, TB:TB + 30]),
                     reads=[f"h{j}" for j in range(8)], writes=[f"h{j}" for j in range(8)])
            for j in range(8):
                pa, pg = j % 2, 2 + j % 2
                for k in range(KC):
                    P.op("pe", lambda e, k=k, j=j, pa=pa, xb=xb: e.matmul(ps[pa], lhsT=W1[:, k, j * 128:(j + 1) * 128], rhs=xb[:, k, :], start=(k == 0), stop=(k == KC - 1)),
                         reads=[f"{tag}{k}"], writes=[f"ps{pa}"], signal=(k == KC - 1))
                for k in range(KC):
                    P.op("pe", lambda e, k=k, j=j, pg=pg, xb=xb: e.matmul(ps[pg], lhsT=W1[:, k, D + j * 128:D + (j + 1) * 128], rhs=xb[:, k, :], start=(k == 0), stop=(k == KC - 1)),
                         reads=[f"{tag}{k}"], writes=[f"ps{pg}"], signal=(k == KC - 1))
                P.op("act", lambda e, j=j, pg=pg: e.activation(out=sg[j % 2], in_=ps[pg], func=AF.Sigmoid, bias=vcol(("pw1b", l), 8 + j)),
                     reads=[f"ps{pg}"], writes=[f"trot{2 + j % 2}"])
                P.op("dve", lambda e, j=j, pa=pa: e.scalar_tensor_tensor(out=hT[:, j, 30:30 + TB], in0=ps[pa], scalar=vcol(("pw1b", l), j), in1=sg[j % 2], op0=ALU.add, op1=ALU.mult),
                     reads=[f"ps{pa}", f"trot{2 + j % 2}"], writes=[f"h{j}"])
            df = Defer()
            for j in range(8):
                pc = 4 + j % 2
                for k in range(CW):
                    P.op("pe", lambda e, k=k, j=j, pc=pc: e.matmul(ps[pc], lhsT=DG[:, k * 8 + j, :], rhs=hT[:, j, k:k + TB], start=(k == 0), stop=(k == CW - 1)),
                         reads=[f"h{j}", f"dg{k * 8 + j}"], writes=[f"ps{pc}"], signal=(k == CW - 1))
                df.flush()
                P.op("act", lambda e, j=j, pc=pc: e.activation(out=yT[:, j, :].bitcast(F32R), in_=ps[pc], func=AF.Identity, bias=vcol(("dwb", l), j)),
                     reads=[f"ps{pc}"], writes=[f"y{j}"])
                stat_sq(j, yT[:, j, :], f"y{j}", sqrot)
                df.f = (lambda j=j: stat_mm(j, yT[:, j, :], f"y{j}", sqrot, j == 7))
            df.flush()
            stats_finish(6, 7, mu, rstd, nmr)
            for j in range(8):
                i = j % 2
                P.op("dve", lambda e, o=trot[i], s=yT[:, j, :]: e.tensor_tensor(out=o, in0=s, in1=rstd, op=ALU.mult),
                     reads=[f"y{j}", "rstd"], writes=[f"trot{i}"])
                P.op("dve", lambda e, o=trot[i]: e.tensor_tensor(out=o, in0=o, in1=nmr, op=ALU.add),
                     reads=[f"trot{i}", "nmr"], writes=[f"trot{i}"])
                P.op("act", lambda e, o=xb[:, j, :], s=trot[i], j=j: e.activation(out=o, in_=s, func=AF.Silu, bias=vcol(("lnb", l), j), scale=vcol(("lng", l), j)),
                     reads=[f"trot{i}"], writes=[f"{tag}{j}"])
            for j in range(8):
                pm = 4 + j % 2
                i = j % 2
                for k in range(KC):
                    P.op("pe", lambda e, k=k, j=j, pm=pm, xb=xb: e.matmul(ps[pm], lhsT=W2[:, k, j * 128:(j + 1) * 128], rhs=xb[:, k, :], start=(k == 0), stop=(k == KC - 1)),
                         reads=[f"{tag}{k}"], writes=[f"ps{pm}"], signal=(k == KC - 1))
                df.flush()
                P.op("act", lambda e, j=j, pm=pm, i=i: e.activation(out=sg[i], in_=ps[pm], func=AF.Identity, bias=vcol(("pw2b", l), j)),
                     reads=[f"ps{pm}"], writes=[f"trot{2 + i}"])
                P.dma("sp", xrot[2 + i], Xin[j * 128:(j + 1) * 128, b * TB:(b + 1) * TB], f"xrot{2 + i}", writes=[f"xrot{2 + i}"])
                P.op("dve", lambda e, j=j, i=i: e.scalar_tensor_tensor(out=yT[:, j, :].bitcast(F32R), in0=xrot[2 + i], scalar=ALPHA, in1=sg[i], op0=ALU.mult, op1=ALU.add),
                     reads=[f"xrot{2 + i}", f"trot{2 + i}"], writes=[f"y{j}"])
                stat_sq(j, yT[:, j, :], f"y{j}", sqrot)
                df.f = (lambda j=j: stat_mm(j, yT[:, j, :], f"y{j}", sqrot, j == 7))
            df.flush()
            stats_finish(6, 7, mu, rstd, nmr)
            ln_out(yT, rstd, nmr, trot, orot, ("mixg", l), ("mixb", l), Xout, b)
        P.barrier()

    def phase_F1(l, Xin):
        arena.reset()
        Wu = arena.take([128, KC, FF], BF16, 2)
        Wg = arena.take([128, KC, FF], BF16, 2)
        stg = [arena.take([128, 1024], F32, 4) for _ in range(4)]
        xrot = [arena.take([128, TB], F32, 4) for _ in range(2)]
        xbs = [arena.take([128, KC, TB], BF16, 2) for _ in range(2)]
        gs = [arena.take([128, TB + 2], F32, 4) for _ in range(3)]
        acc = [arena.take([128, TB], F32, 4) for _ in range(3)]
        hrot = [arena.take([128, TB], BF16, 2) for _ in range(3)]
        gprev = arena.take([128, FC, 2], F32, 4)
        load_weight(Wu, wd["ffn_w_up"][l], D, FF, stg)
        load_weight(Wg, wd["ffn_w_gate"][l], D, FF, stg)
        P.op("dve", lambda e: e.tensor_copy(out=gprev.rearrange("p a b -> p (a b)"), in_=zeros32[:, 0:2 * FC]), writes=["gprev"])
        P.barrier()
        load_xb(Xin, 0, xrot, xbs[0], "xb0_")
        cnt = 0
        for b in range(NBLK):
            xb = xbs[b % 2]
            tag = f"xb{b % 2}_"
            for f in range(FC):
                if b + 1 < NBLK and f <= KC:
                    if f < KC:
                        xb_dma(Xin, b + 1, xrot, f)
                    if f >= 1:
                        xb_cast(xrot, xbs[(b + 1) % 2], f"xb{(b + 1) % 2}_", f - 1)
                i = cnt % 3
                cnt += 1
                pu, pg = i, 3 + i
                for k in range(KC):
                    P.op("pe", lambda e, k=k, f=f, pu=pu, xb=xb: e.matmul(ps[pu], lhsT=Wu[:, k, f * 128:(f + 1) * 128], rhs=xb[:, k, :], start=(k == 0), stop=(k == KC - 1)),
                         reads=[f"{tag}{k}"], writes=[f"ps{pu}"], signal=(k == KC - 1))
                for k in range(KC):
                    P.op("pe", lambda e, k=k, f=f, pg=pg, xb=xb: e.matmul(ps[pg], lhsT=Wg[:, k, f * 128:(f + 1) * 128], rhs=xb[:, k, :], start=(k == 0), stop=(k == KC - 1)),
                         reads=[f"{tag}{k}"], writes=[f"ps{pg}"], signal=(k == KC - 1))
                P.op("act", lambda e, i=i, pg=pg: e.activation(out=gs[i][:, 2:2 + TB], in_=ps[pg], func=AF.Copy), reads=[f"ps{pg}"], writes=[f"gs{i}"])
                P.op("dve", lambda e, i=i, f=f: e.tensor_copy(out=gs[i][:, 0:2], in_=gprev[:, f, :]), reads=["gprev"], writes=[f"gs{i}"])
                P.op("act", lambda e, i=i, pg=pg, f=f: e.activation(out=acc[i], in_=ps[pg], func=AF.Identity, bias=vcol(("fcb", l), f), scale=vcol(("fcw", l), 2 * FC + f)),
                     reads=[f"ps{pg}"], writes=[f"acc{i}"])
                P.op("dve", lambda e, i=i, f=f: e.scalar_tensor_tensor(out=acc[i], in0=gs[i][:, 1:1 + TB], scalar=vcol(("fcw", l), FC + f), in1=acc[i], op0=ALU.mult, op1=ALU.add),
                     reads=[f"gs{i}", f"acc{i}"], writes=[f"acc{i}"])
                P.op("dve", lambda e, i=i, f=f: e.scalar_tensor_tensor(out=acc[i], in0=gs[i][:, 0:TB], scalar=vcol(("fcw", l), f), in1=acc[i], op0=ALU.mult, op1=ALU.add),
                     reads=[f"gs{i}", f"acc{i}"], writes=[f"acc{i}"])
                P.op("dve", lambda e, i=i, f=f: e.tensor_copy(out=gprev[:, f, :], in_=gs[i][:, TB:TB + 2]), reads=[f"gs{i}"], writes=["gprev"])
                P.op("act", lambda e, i=i: e.activation(out=acc[i], in_=acc[i], func=AF.Silu), reads=[f"acc{i}"], writes=[f"acc{i}"])
                P.op("dve", lambda e, i=i, pu=pu: e.tensor_tensor(out=hrot[i], in0=acc[i], in1=ps[pu], op=ALU.mult),
                     reads=[f"acc{i}", f"ps{pu}"], writes=[f"hrot{i}"])
                P.dma("pool", HH[f * 128:(f + 1) * 128, b * TB:(b + 1) * TB], hrot[i], f"hrot{i}", reads=[f"hrot{i}"])
        P.barrier()

    def phase_F2(l, Xin, Xout):
        arena.reset()
        Wd = arena.take([128, FC, D], BF16, 2)
        Wq = arena.take([128, KC, D], BF16, 2)
        Wp = arena.take([128, 2, D], BF16, 2)
        stg = [arena.take([128, 512], F32, 4) for _ in range(2)]
        xrot = [arena.take([128, TB], F32, 4) for _ in range(4)]
        xbs = [arena.take([128, KC, TB], BF16, 2) for _ in range(2)]
        hhs = [arena.take([128, FC, TB], BF16, 2) for _ in range(2)]
        p32 = arena.take([128, 2, TB], F32, 4)
        pbs = [arena.take([128, 2, TB], BF16, 2) for _ in range(2)]
        yT, sqrot, trot, orot, mu, rstd, nmr = ln_bufs()
        sgq = trot[0:2]
        load_weight(Wd, wd["ffn_w_down"][l], FF, D, stg)
        load_weight(Wq, wd["ple_w_gate"][l], D, D, stg)
        load_weight(Wp, wd["ple_w_proj"][l], PLE, D, stg)
        P.barrier()

        def load_in(b):
            par = b % 2
            load_xb(Xin, b, xrot, xbs[par], f"xb{par}_")
            P.dma("sp", hhs[par][:, 0:11, :], blk(HH, b)[:, 0:11, :], f"hh{par}_0", writes=[f"hh{par}_0"])
            P.dma("sp", hhs[par][:, 11:22, :], blk(HH, b)[:, 11:22, :], f"hh{par}_1", writes=[f"hh{par}_1"])
            P.dma("sp", p32, blk(pT_d[l], b), "p32", writes=["p32"])
            P.op("pool", lambda e, o=pbs[par]: e.tensor_copy(out=o, in_=p32), reads=["p32"], writes=[f"pb{par}"])

        load_in(0)
        for b in range(NBLK):
            par = b % 2
            xb, hh, pb = xbs[par], hhs[par], pbs[par]
            tag = f"xb{par}_"
            if b + 1 < NBLK:
                load_in(b + 1)
            df = Defer()
            for j in range(8):
                i = j % 2
                pf, pq, pr = i, 2 + i, 4 + i
                for f in range(FC):
                    P.op("pe", lambda e, f=f, j=j, pf=pf, hh=hh: e.matmul(ps[pf], lhsT=Wd[:, f, j * 128:(j + 1) * 128], rhs=hh[:, f, :], start=(f == 0), stop=(f == FC - 1)),
                         reads=[f"hh{par}_{f // 11}"], writes=[f"ps{pf}"], signal=(f == FC - 1))
                for k in range(KC):
                    P.op("pe", lambda e, k=k, j=j, pq=pq, xb=xb: e.matmul(ps[pq], lhsT=Wq[:, k, j * 128:(j + 1) * 128], rhs=xb[:, k, :], start=(k == 0), stop=(k == KC - 1)),
                         reads=[f"{tag}{k}"], writes=[f"ps{pq}"], signal=(k == KC - 1))
                for c in range(2):
                    P.op("pe", lambda e, c=c, j=j, pr=pr, pb=pb: e.matmul(ps[pr], lhsT=Wp[:, c, j * 128:(j + 1) * 128], rhs=pb[:, c, :], start=(c == 0), stop=(c == 1)),
                         reads=[f"pb{par}"], writes=[f"ps{pr}"], signal=(c == 1))
                df.flush()
                P.op("act", lambda e, i=i, pq=pq: e.activation(out=sgq[i], in_=ps[pq], func=AF.Sigmoid), reads=[f"ps{pq}"], writes=[f"trot{i}"])
                P.op("dve", lambda e, i=i, pr=pr: e.tensor_tensor(out=sgq[i], in0=sgq[i], in1=ps[pr], op=ALU.mult), reads=[f"trot{i}", f"ps{pr}"], writes=[f"trot{i}"])
                P.op("dve", lambda e, i=i, pf=pf: e.tensor_tensor(out=sgq[i], in0=sgq[i], in1=ps[pf], op=ALU.add), reads=[f"trot{i}", f"ps{pf}"], writes=[f"trot{i}"])
                P.dma("sp", xrot[2 + i], Xin[j * 128:(j + 1) * 128, b * TB:(b + 1) * TB], f"xrot{2 + i}", writes=[f"xrot{2 + i}"])
                P.op("dve", lambda e, i=i, j=j: e.scalar_tensor_tensor(out=yT[:, j, :].bitcast(F32R), in0=xrot[2 + i], scalar=ALPHA, in1=sgq[i], op0=ALU.mult, op1=ALU.add),
                     reads=[f"xrot{2 + i}", f"trot{i}"], writes=[f"y{j}"])
                stat_sq(j, yT[:, j, :], f"y{j}", sqrot)
                df.f = (lambda j=j: stat_mm(j, yT[:, j, :], f"y{j}", sqrot, j == 7))
            df.flush()
            stats_finish(6, 7, mu, rstd, nmr)
            ln_out(yT, rstd, nmr, trot, orot, ("ffg", l), ("ffb", l), Xout, b)
        P.barrier()

    def phase_KVQ(Xin, lq, do_kv):
        arena.reset()
        Wq = arena.take([128, KC, D], BF16, 2)
        if do_kv:
            Wk = arena.take([128, KC, D], BF16, 2)
            Wv = arena.take([128, KC, D], BF16, 2)
        stg = [arena.take([128, 1024], F32, 4) for _ in range(4)]
        xrot = [arena.take([128, TB], F32, 4) for _ in range(2)]
        xbs = [arena.take([128, KC, TB], BF16, 2) for _ in range(2)]
        krot = [arena.take([128, TB], BF16, 2) for _ in range(3)]
        load_weight(Wq, wd["b_wq"][lq], D, D, stg)
        if do_kv:
            load_weight(Wk, wd["kv_wk"], D, D, stg)
            load_weight(Wv, wd["kv_wv"], D, D, stg)
        P.barrier()
        cnt = 0
        load_xb(Xin, 0, xrot, xbs[0], "xb0_")
        for b in range(NBLK):
            xb = xbs[b % 2]
            tag = f"xb{b % 2}_"
            if b + 1 < NBLK:
                load_xb(Xin, b + 1, xrot, xbs[(b + 1) % 2], f"xb{(b + 1) % 2}_")
            for W, dst, scale in ([(Wq, QT, 0.125)] + ([(Wk, KT, 1.0)] if do_kv else [])):
                for j in range(8):
                    i = cnt % 3
                    cnt += 1
                    for k in range(KC):
                        P.op("pe", lambda e, k=k, j=j, i=i, W=W, xb=xb: e.matmul(ps[i], lhsT=W[:, k, j * 128:(j + 1) * 128], rhs=xb[:, k, :], start=(k == 0), stop=(k == KC - 1)),
                             reads=[f"{tag}{k}"], writes=[f"ps{i}"], signal=(k == KC - 1))
                    P.op("act", lambda e, i=i, scale=scale: e.activation(out=krot[i], in_=ps[i], func=AF.Copy, scale=scale), reads=[f"ps{i}"], writes=[f"krot{i}"])
                    P.dma("pool", dst[j * 128:(j + 1) * 128, b * TB:(b + 1) * TB], krot[i], f"krot{i}", reads=[f"krot{i}"])
            if do_kv:
                for t in range(4):
                    for hf in range(2):
                        i = cnt % 3
                        cnt += 1
                        for k in range(KC):
                            P.op("pe", lambda e, k=k, t=t, hf=hf, i=i, xb=xb: e.matmul(ps[i], lhsT=xb[:, k, t * 128:(t + 1) * 128], rhs=Wv[:, k, hf * 512:(hf + 1) * 512], start=(k == 0), stop=(k == KC - 1)),
                                 reads=[f"{tag}{k}"], writes=[f"ps{i}"], signal=(k == KC - 1))
                        P.op("dve", lambda e, i=i: e.tensor_copy(out=krot[i], in_=ps[i]), reads=[f"ps{i}"], writes=[f"krot{i}"])
                        P.dma("pool", VV[b * TB + t * 128:b * TB + (t + 1) * 128, hf * 512:(hf + 1) * 512], krot[i], f"krot{i}", reads=[f"krot{i}"])
        P.barrier()

    def phase_ATT():
        arena.reset()
        Kh = [arena.take([128, S], BF16, 2) for _ in range(2)]
        Qh = [arena.take([128, S], BF16, 2) for _ in range(2)]
        Vh = [arena.take([128, S // 128, 128], BF16, 2) for _ in range(2)]
        for t in (Kh, Qh):
            for i in range(2):
                for c in range(S // 512):
                    P.op("dve", lambda e, o=t[i][64:128, c * 512:(c + 1) * 512]: e.tensor_copy(out=o, in_=zeros32[64:128, :]), writes=["zpad"])
        for i in range(2):
            for c in range(4):
                P.op("pool", lambda e, o=Vh[i][:, c * 8:(c + 1) * 8, 64:128]: e.tensor_copy(out=o, in_=zeros32.rearrange("p (a b) -> p a b", a=8)), writes=["zpad2"])
        mstage = arena.take([128, 2048], F32, 4)
        mask16 = arena.take([128, 4, 512], F16, 2)
        maskb = arena.take([128, 4, 512], BF16, 2)
        P.dma("sp", mstage, consts_d[:, C_MASK:C_MASK + 2048], "mstage", writes=["mstage"])
        P.op("dve", lambda e: e.tensor_copy(out=mask16.rearrange("p a b -> p (a b)"), in_=mstage), reads=["mstage"], writes=["k4"])
        P.op("act", lambda e: e.activation(out=maskb.rearrange("p a b -> p (a b)"), in_=mstage, func=AF.Copy), reads=["mstage"], writes=["k5"])
        P.barrier()
        e32p = [arena.take([128, 2 * TB], F32, 4) for _ in range(3)]
        ec32 = [arena.take([128, TB], F32, 4) for _ in range(2)]
        sp16p = [arena.take([128, 2 * TB], F16, 2) for _ in range(3)]
        S16 = [arena.take([128, TB], F16, 2) for _ in range(2)]
        a16 = [arena.take([128, TB], BF16, 2) for _ in range(4)]
        orot = [arena.take([64, TB], BF16, 2) for _ in range(2)]
        items = []
        for h in range(NH):
            for qb in range(NBLK):
                kts = list(range(qb * 4 + 3, -1, -1))
                for n, kt in enumerate(kts):
                    items.append((h, qb, kt, n == 0, n == len(kts) - 1, kt - qb * 4))
        NI = len(items)
        NP = NI // 2

        def load_head(h):
            hi = h % 2
            P.dma("sp", Kh[hi][0:DH, :], KT[h * DH:(h + 1) * DH, :], f"Kh{hi}", writes=[f"Kh{hi}"])
            P.dma("sp", Qh[hi][0:DH, :], QT[h * DH:(h + 1) * DH, :], f"Qh{hi}", writes=[f"Qh{hi}"])
            for c in range(4):
                P.dma("sp", Vh[hi][:, c * 8:(c + 1) * 8, 0:DH], VV.rearrange("(t p) d -> p t d", p=128)[:, c * 8:(c + 1) * 8, h * DH:(h + 1) * DH],
                      f"Vh{hi}_{c}", writes=[f"Vh{hi}_{c}"])

        st = {"si": 0}

        def stage_A(p):
            pb, eb = p % 2, p % 3
            for u in range(2):
                h, qb, kt, first, last, d = items[2 * p + u]
                hi = h % 2
                P.op("pe", lambda e, hi=hi, kt=kt, qb=qb, u=u: e.matmul(pp[pb][:, u * TB:(u + 1) * TB], lhsT=Kh[hi][:, kt * 128:(kt + 1) * 128], rhs=Qh[hi][:, qb * TB:(qb + 1) * TB], start=True, stop=True),
                     reads=[f"Kh{hi}", f"Qh{hi}"], writes=[f"pz{pb}"], signal=True)
            P.op("act", lambda e: e.activation(out=e32p[eb], in_=pp[pb], func=AF.Exp), reads=[f"pz{pb}"], writes=[f"e{eb}"])
            P.op("act", lambda e: e.activation(out=sp16p[eb], in_=e32p[eb], func=AF.Ln, bias=1.0), reads=[f"e{eb}"], writes=[f"sp{eb}"])
            for u in range(2):
                d = items[2 * p + u][5]
                if d >= 0:
                    P.op("pool", lambda e, u=u, d=d: e.tensor_tensor(out=sp16p[eb][:, u * TB:(u + 1) * TB], in0=sp16p[eb][:, u * TB:(u + 1) * TB], in1=mask16[:, d, :], op=ALU.mult),
                         reads=[f"sp{eb}"], writes=[f"sp{eb}"])

        def stage_B(t):
            h, qb, kt, first, last, d = items[t]
            eb, u = (t // 2) % 3, t % 2
            e32 = e32p[eb][:, u * TB:(u + 1) * TB]
            sp16 = sp16p[eb][:, u * TB:(u + 1) * TB]
            i4, j = t % 4, t % 2
            pc = 4 + j
            si = st["si"]
            P.op("pe", lambda e: e.matmul(ps[pc], lhsT=tri16, rhs=sp16, start=True, stop=first),
                 reads=[f"sp{eb}"], writes=[f"ps{pc}"], signal=first)
            if not first:
                P.op("pe", lambda e: e.matmul(ps[pc], lhsT=ones16, rhs=S16[si], start=False, stop=True),
                     reads=[f"S{si}"], writes=[f"ps{pc}"], signal=True)
            if not last:
                if first:
                    P.op("dve", lambda e: e.tensor_copy(out=S16[0], in_=sp16), reads=[f"sp{eb}"], writes=["S0"])
                    st["si"] = 0
                else:
                    P.op("dve", lambda e: e.tensor_tensor(out=S16[1 - si], in0=S16[si], in1=sp16, op=ALU.add),
                         reads=[f"sp{eb}", f"S{si}"], writes=[f"S{1 - si}"])
                    st["si"] = 1 - si
            P.op("act", lambda e: e.activation(out=ec32[j], in_=ps[pc], func=AF.Exp, scale=-1.0), reads=[f"ps{pc}"], writes=[f"ec{j}"])
            P.op("dve", lambda e: e.tensor_tensor(out=a16[i4], in0=e32, in1=ec32[j], op=ALU.mult), reads=[f"e{eb}", f"ec{j}"], writes=[f"a{i4}"])
            if d >= 0:
                P.op("pool", lambda e: e.tensor_tensor(out=a16[i4], in0=a16[i4], in1=maskb[:, d, :], op=ALU.mult), reads=[f"a{i4}"], writes=[f"a{i4}"])

        def stage_C(t):
            h, qb, kt, first, last, d = items[t]
            hi, i4 = h % 2, t % 4
            hq = h * NBLK + qb
            po = 6 + hq % 2
            P.op("pe", lambda e: e.matmul(ps[po], lhsT=Vh[hi][:, kt, :], rhs=a16[i4], start=first, stop=last),
                 reads=[f"a{i4}", f"Vh{hi}_{kt // 8}"], writes=[f"ps{po}"], signal=last)
            if last:
                oi = hq % 2
                P.op("dve", lambda e: e.tensor_copy(out=orot[oi], in_=ps[po][0:DH, :]), reads=[f"ps{po}"], writes=[f"orot{oi}"])
                P.dma("pool", OT[h * DH:(h + 1) * DH, qb * TB:(qb + 1) * TB], orot[oi], f"orot{oi}", reads=[f"orot{oi}"])
                if qb == NBLK - 1 and h + 2 < NH:
                    load_head(h + 2)

        load_head(0)
        load_head(1)
        for p in range(NP + 2):
            if p < NP:
                stage_A(p)
            if 0 <= p - 1 < NP:
                stage_B(2 * (p - 1))
                stage_B(2 * (p - 1) + 1)
            if 0 <= p - 2 < NP:
                stage_C(2 * (p - 2))
                stage_C(2 * (p - 2) + 1)
        P.barrier()

    def phase_WO(l, Xin, Xout):
        arena.reset()
        Wo = arena.take([128, KC, D], BF16, 2)
        stg = [arena.take([128, 1024], F32, 4) for _ in range(4)]
        xrot = [arena.take([128, TB], F32, 4) for _ in range(4)]
        xbs = [arena.take([128, KC, TB], BF16, 2) for _ in range(2)]
        yT, sqrot, trot, orot, mu, rstd, nmr = ln_bufs()
        load_weight(Wo, wd["b_wo"][l - NA], D, D, stg)
        P.barrier()
        P.dma("sp", xbs[0], blk(OT, 0), "xbfull0", writes=[f"xb0_{k}" for k in range(KC)])
        for b in range(NBLK):
            xb = xbs[b % 2]
            tag = f"xb{b % 2}_"
            if b + 1 < NBLK:
                P.dma("sp", xbs[(b + 1) % 2], blk(OT, b + 1), f"xbfull{(b + 1) % 2}", writes=[f"xb{(b + 1) % 2}_{k}" for k in range(KC)])
            df = Defer()
            for j in range(8):
                pm = 4 + j % 2
                i = j % 2
                for k in range(KC):
                    P.op("pe", lambda e, k=k, j=j, pm=pm, xb=xb: e.matmul(ps[pm], lhsT=Wo[:, k, j * 128:(j + 1) * 128], rhs=xb[:, k, :], start=(k == 0), stop=(k == KC - 1)),
                         reads=[f"{tag}{k}"], writes=[f"ps{pm}"], signal=(k == KC - 1))
                df.flush()
                P.dma("sp", xrot[2 + i], Xin[j * 128:(j + 1) * 128, b * TB:(b + 1) * TB], f"xrot{2 + i}", writes=[f"xrot{2 + i}"])
                P.op("dve", lambda e, j=j, i=i, pm=pm: e.scalar_tensor_tensor(out=yT[:, j, :].bitcast(F32R), in0=xrot[2 + i], scalar=ALPHA, in1=ps[pm], op0=ALU.mult, op1=ALU.add),
                     reads=[f"xrot{2 + i}", f"ps{pm}"], writes=[f"y{j}"])
                stat_sq(j, yT[:, j, :], f"y{j}", sqrot)
                df.f = (lambda j=j: stat_mm(j, yT[:, j, :], f"y{j}", sqrot, j == 7))
            df.flush()
            stats_finish(6, 7, mu, rstd, nmr)
            ln_out(yT, rstd, nmr, trot, orot, ("mixg", l), ("mixb", l), Xout, b)
        P.barrier()

    cur = xT_d
    for l in range(nlayers):
        last = (l == nlayers - 1)
        if l < NA:
            phase_C(l, cur, XA)
        else:
            phase_KVQ(cur, l - NA, do_kv=(l == NA))
            phase_ATT()
            phase_WO(l, cur, XA)
        phase_F1(l, XA)
        nxt = out_d if last else XB
        phase_F2(l, XA, nxt)
        cur = nxt
    with nc.Block() as block:
        P.replay(block)
    return nc


_NC_CACHE = {}


def kernel(**inputs):
    inp = {k: np.asarray(v) for k, v in inputs.items()}
    if "nc" not in _NC_CACHE:
        _NC_CACHE["nc"] = build(DEPTH)
    nc = _NC_CACHE["nc"]
    vecs = host_vecs(inp)
    consts = host_consts()
    shared = {"vecs": vecs, "consts": consts}
    for name in ("a_pw1_w", "a_pw2_w", "b_wq", "kv_wk", "kv_wv", "b_wo", "ffn_w_up", "ffn_w_gate",
                 "ffn_w_down", "ple_w_gate", "ple_w_proj"):
        shared[name] = np.ascontiguousarray(inp[name], dtype=np.float32)
    in_maps = []
    for c in range(NCORES):
        m = dict(shared)
        m["xT"] = np.ascontiguousarray(inp["x"][c].T)
        m["pT"] = np.ascontiguousarray(np.transpose(inp["p"][:, c], (0, 2, 1)))
        in_maps.append(m)
    res = run_bass_kernel_spmd(nc, in_maps, core_ids=list(range(NCORES)))
    out = np.stack([np.asarray(r["outT"]).T for r in res.results], axis=0)
    return np.ascontiguousarray(out.astype(np.float32))
```

```python
import numpy as np
import concourse.bass as bass
import concourse.mybir as mybir
from concourse.bass_utils import run_bass_kernel_spmd

F32 = mybir.dt.float32
BF16 = mybir.dt.bfloat16
F16 = mybir.dt.float16
F32R = mybir.dt.float32r
U8 = mybir.dt.uint8
AF = mybir.ActivationFunctionType
ALU = mybir.AluOpType

D = 1024
S = 4096
FF = 2816
DEPTH = 4
NA = 2
NH = 16
DH = 64
CW = 31
PLE = 256
TB = 512
NBLK = S // TB
KC = D // 128
FC = FF // 128
ALPHA = float((2.0 * DEPTH) ** 0.25)
EPS = 1e-5
NCORES = 8

ENG = ("pe", "act", "dve", "pool", "sp")


class Tok:
    __slots__ = ("sem", "val", "eng")

    def __init__(self, sem=None, val=None, eng=None):
        self.sem = sem
        self.val = val
        self.eng = eng


class Prog:
    def __init__(self, nc):
        self.nc = nc
        self.thunks = {e: [] for e in ENG}
        self.psem = {e: nc.alloc_semaphore("prog_" + e) for e in ENG}
        self.pcnt = {e: 0 for e in ENG}
        self.W = {}
        self.R = {}
        self.pe_pending = []
        self.dsem = {}

    def _deps(self, reads, writes):
        deps = []
        for r in reads:
            t = self.W.get(r)
            if t is not None:
                deps.append(t)
        for w in writes:
            t = self.W.get(w)
            if t is not None:
                deps.append(t)
            deps.extend(self.R.get(w, ()))
        return deps

    def _commit(self, tok, reads, writes):
        for r in reads:
            self.R.setdefault(r, []).append(tok)
        for w in writes:
            self.W[w] = tok
            self.R[w] = []

    def op(self, eng, fn, reads=(), writes=(), signal=None):
        if signal is None:
            signal = eng != "pe"
        deps = self._deps(reads, writes)
        if eng == "pe":
            deps = [d for d in deps if d.eng != "pe"]
        tok = Tok(eng=eng)
        if signal:
            self.pcnt[eng] += 1
            tok.sem = self.psem[eng]
            tok.val = self.pcnt[eng]
            if eng == "pe":
                for p in self.pe_pending:
                    p.sem = tok.sem
                    p.val = tok.val
                self.pe_pending = []
        else:
            self.pe_pending.append(tok)
        self._commit(tok, reads, writes)
        self.thunks[eng].append((deps, fn, self.psem[eng] if signal else None, 1))
        return tok

    def dma(self, eng, out, in_, slot, reads=(), writes=()):
        ds = self.dsem.get(slot)
        if ds is None:
            ds = [self.nc.alloc_semaphore("d_" + slot), 0]
            self.dsem[slot] = ds
        deps = self._deps(reads, writes)
        if ds[1] > 0:
            deps.append(Tok(ds[0], ds[1]))
        ds[1] += 16
        tok = Tok(ds[0], ds[1])
        self._commit(tok, reads, writes)

        def fn(e, out=out, in_=in_):
            return e.dma_start(out=out, in_=in_)
        self.thunks[eng].append((deps, fn, ds[0], 16))
        return tok

    def barrier(self):
        assert not self.pe_pending
        toks = [Tok(ds[0], ds[1]) for ds in self.dsem.values() if ds[1] > 0]
        toks += [Tok(self.psem[x], self.pcnt[x]) for x in ENG if self.pcnt[x] > 0]
        for e in ENG:
            self.thunks[e].append((list(toks), None, None, 0))
        self.W = {}
        self.R = {}

    def replay(self, block):
        names = {"pe": "tensor", "act": "scalar", "dve": "vector", "pool": "gpsimd", "sp": "sync"}
        for e in ENG:
            thunks = self.thunks[e]
            if not thunks:
                continue

            def body(h, thunks=thunks):
                waited = {}
                for deps, fn, sem, inc in thunks:
                    for d in deps:
                        assert d.sem is not None, "unresolved PE token"
                        k = id(d.sem)
                        if waited.get(k, 0) >= d.val:
                            continue
                        waited[k] = d.val
                        h.wait_ge(d.sem, d.val)
                    if fn is None:
                        continue
                    ins = fn(h)
                    if sem is not None:
                        ins.then_inc(sem, inc)
            getattr(block, names[e])(body)


def vec_layout():
    off = {}
    n = 0

    def add(key, cnt):
        nonlocal n
        off[key] = n
        n += cnt
    for l in range(NA):
        add(("pw1b", l), 16)
        add(("dwb", l), 8)
        add(("lng", l), 8)
        add(("lnb", l), 8)
        add(("pw2b", l), 8)
        add(("dww", l), CW * 8)
    for l in range(DEPTH):
        add(("mixg", l), 8)
        add(("mixb", l), 8)
        add(("fcw", l), 3 * FC)
        add(("fcb", l), FC)
        add(("ffg", l), 8)
        add(("ffb", l), 8)
    return off, n


VOFF, NV = vec_layout()
C_ONES, C_TRI, C_ID, C_ZERO, C_MASK, NCONST = 0, 128, 256, 384, 896, 896 + 2048


def host_consts():
    c = np.zeros((128, NCONST), np.float32)
    c[:, C_ONES:C_ONES + 128] = 1.0
    j = np.arange(128)[:, None]
    s = np.arange(128)[None, :]
    c[:, C_TRI:C_TRI + 128] = (j >= s).astype(np.float32)
    c[:, C_ID:C_ID + 128] = np.eye(128, dtype=np.float32)
    i = np.arange(512)[None, :]
    for d in range(4):
        c[:, C_MASK + d * 512:C_MASK + (d + 1) * 512] = ((i - 128 * d) > j).astype(np.float32)
    return c


def host_vecs(inp):
    v = np.zeros((128, NV), np.float32)

    def put(key, vec):
        vec = np.asarray(vec, np.float32)
        n = vec.shape[0] // 128
        v[:, VOFF[key]:VOFF[key] + n] = vec.reshape(n, 128).T
    for l in range(NA):
        put(("pw1b", l), inp["a_pw1_b"][l])
        put(("dwb", l), inp["a_dw_b"][l])
        put(("lng", l), inp["a_ln_g"][l])
        put(("lnb", l), inp["a_ln_b"][l])
        put(("pw2b", l), inp["a_pw2_b"][l])
        put(("dww", l), np.asarray(inp["a_dw_w"][l]).reshape(-1))
    for l in range(DEPTH):
        put(("mixg", l), inp["ln_mix_g"][l])
        put(("mixb", l), inp["ln_mix_b"][l])
        put(("fcw", l), np.asarray(inp["ffn_conv_w"][l]).reshape(-1))
        put(("fcb", l), inp["ffn_conv_b"][l])
        put(("ffg", l), inp["ln_ffn_g"][l])
        put(("ffb", l), inp["ln_ffn_b"][l])
    return v


class Arena:
    def __init__(self, ap, nbytes):
        self.ap = ap
        self.n = nbytes
        self.off = 0

    def reset(self):
        self.off = 0

    def take(self, shape, dt, esz):
        nb = esz * int(np.prod(shape[1:]))
        n = (nb + 63) // 64 * 64
        assert self.off + n <= self.n, ("arena overflow", self.off + n, self.n)
        v = self.ap[0:shape[0], self.off:self.off + nb].bitcast(dt)
        self.off += n
        if len(shape) == 3:
            v = v.rearrange("p (a b) -> p a b", a=shape[1])
        return v


def build(nlayers=DEPTH):
    nc = bass.Bass("TRN2", target_bir_lowering=False)
    dr = lambda name, shape, dt, kind: nc.dram_tensor(name, shape, dt, kind=kind).ap()
    xT_d = dr("xT", [D, S], F32, "ExternalInput")
    pT_d = dr("pT", [DEPTH, PLE, S], F32, "ExternalInput")
    vecs_d = dr("vecs", [128, NV], F32, "ExternalInput")
    consts_d = dr("consts", [128, NCONST], F32, "ExternalInput")
    wd = {}
    for name, shape in (("a_pw1_w", [NA, D, 2 * D]), ("a_pw2_w", [NA, D, D]), ("b_wq", [2, D, D]),
                        ("kv_wk", [D, D]), ("kv_wv", [D, D]), ("b_wo", [2, D, D]),
                        ("ffn_w_up", [DEPTH, D, FF]), ("ffn_w_gate", [DEPTH, D, FF]),
                        ("ffn_w_down", [DEPTH, FF, D]), ("ple_w_gate", [DEPTH, D, D]),
                        ("ple_w_proj", [DEPTH, PLE, D])):
        wd[name] = dr(name, shape, F32, "ExternalInput")
    out_d = dr("outT", [D, S], F32, "ExternalOutput")
    XA = dr("XA", [D, S], F32, "Internal")
    XB = dr("XB", [D, S], F32, "Internal")
    HH = dr("HH", [FF, S], BF16, "Internal")
    KT = dr("KT", [D, S], BF16, "Internal")
    VV = dr("VV", [S, D], BF16, "Internal")
    QT = dr("QT", [D, S], BF16, "Internal")
    OT = dr("OT", [D, S], BF16, "Internal")

    P = Prog(nc)
    sb = lambda name, shape, dt: nc.alloc_sbuf_tensor("sb_" + name, shape, dt).ap()
    vecs = sb("vecs", [128, NV], F32)
    cst = sb("consts", [128, C_MASK], F32)
    ones_r = sb("ones_r", [128, 128], F32)
    idb = sb("idb", [128, 128], BF16)
    tri16 = sb("tri16", [128, 128], F16)
    ones16 = sb("ones16", [128, 128], F16)
    yT_g = sb("yT", [128, KC, TB], F32)
    sqrot_g = [sb(f"sqrot{i}", [128, TB], F32) for i in range(2)]
    ARENA_BYTES = nc.sbuf_bytes_remaining - 1024
    arena = Arena(sb("arena", [128, ARENA_BYTES], U8), ARENA_BYTES)
    pp = [nc.alloc_psum_tensor(f"pp{i}", [128, 1024], F32).ap() for i in range(4)]
    ps = [pp[i // 2][:, (i % 2) * 512:(i % 2 + 1) * 512] for i in range(8)]

    P.dma("sp", vecs, vecs_d, "vecs", writes=["vecs"])
    P.dma("sp", cst, consts_d[:, 0:C_MASK], "consts", writes=["consts"])
    P.op("dve", lambda e: e.tensor_copy(out=ones_r.bitcast(F32R), in_=cst[:, C_ONES:C_ONES + 128]), reads=["consts"], writes=["k"])
    P.op("dve", lambda e: e.tensor_copy(out=idb, in_=cst[:, C_ID:C_ID + 128]), reads=["consts"], writes=["k1"])
    P.op("dve", lambda e: e.tensor_copy(out=tri16, in_=cst[:, C_TRI:C_TRI + 128]), reads=["consts"], writes=["k2"])
    P.op("dve", lambda e: e.tensor_copy(out=ones16, in_=cst[:, C_ONES:C_ONES + 128]), reads=["consts"], writes=["k3"])
    P.barrier()
    zeros32 = cst[:, C_ZERO:C_ZERO + 512]

    def vcol(key, j):
        c = VOFF[key] + j
        return vecs[:, c:c + 1]

    cast_rr = [0]
    CAST_ENG = ("act", "dve", "act", "dve", "act", "pool")

    WCH = {}

    def wr(tag, kc, col):
        return f"W_{tag}_{kc}_{col // WCH[tag]}"

    def load_weights(specs, stg):
        CH = stg[0].shape[1]
        ns = len(stg)
        for (_, _, _, _, tag) in specs:
            WCH[tag] = CH
        ngroups = max((N + CH - 1) // CH for (_, _, _, N, _) in specs)
        for g in range(ngroups):
            c0 = g * CH
            for (dst, src, K, N, tag) in specs:
                if c0 >= N:
                    continue
                cw = min(CH, N - c0)
                for kc in range(K // 128):
                    i = cast_rr[0] % ns
                    eng = CAST_ENG[cast_rr[0] % len(CAST_ENG)]
                    cast_rr[0] += 1
                    P.dma("sp", stg[i][:, 0:cw], src[kc * 128:(kc + 1) * 128, c0:c0 + cw], f"stg{i}", writes=[f"stg{i}"])
                    reg = wr(tag, kc, c0)
                    if eng == "act":
                        P.op("act", lambda e, o=dst[:, kc, c0:c0 + cw], s=stg[i][:, 0:cw]: e.activation(out=o, in_=s, func=AF.Copy),
                             reads=[f"stg{i}"], writes=[reg])
                    else:
                        P.op(eng, lambda e, o=dst[:, kc, c0:c0 + cw], s=stg[i][:, 0:cw]: e.tensor_copy(out=o, in_=s),
                             reads=[f"stg{i}"], writes=[reg])

    def blk(ap2d, b):
        return ap2d.rearrange("(c p) t -> p c t", p=128)[:, :, b * TB:(b + 1) * TB]

    def xb_dma(Xin, b, xrot, j):
        i = j % 2
        P.dma("pool", xrot[i], Xin[j * 128:(j + 1) * 128, b * TB:(b + 1) * TB], f"xrot{i}", writes=[f"xrot{i}"])

    def xb_cast(xrot, xb, tag, j):
        i = j % 2
        P.op("pool", lambda e, o=xb[:, j, :], s=xrot[i]: e.tensor_copy(out=o, in_=s), reads=[f"xrot{i}"], writes=[f"{tag}{j}"])

    def load_xb(Xin, b, xrot, xb, tag):
        for j in range(KC + 1):
            if j < KC:
                xb_dma(Xin, b, xrot, j)
            if j >= 1:
                xb_cast(xrot, xb, tag, j - 1)

    def stats_finish(s1, s2, mu, rstd, nmr):
        P.op("dve", lambda e: e.tensor_scalar(out=mu, in0=ps[s1], scalar1=1.0 / D, scalar2=None, op0=ALU.mult), reads=[f"ps{s1}"], writes=["mu"])
        P.op("dve", lambda e: e.tensor_tensor(out=nmr, in0=mu, in1=mu, op=ALU.mult), reads=["mu"], writes=["nmr"])
        P.op("dve", lambda e: e.scalar_tensor_tensor(out=rstd, in0=ps[s2], scalar=1.0 / D, in1=nmr, op0=ALU.mult, op1=ALU.subtract),
             reads=[f"ps{s2}", "nmr"], writes=["rstd"])
        P.op("act", lambda e: e.activation(out=rstd, in_=rstd, func=AF.Sqrt, bias=EPS), reads=["rstd"], writes=["rstd"])
        P.op("dve", lambda e: e.reciprocal(out=rstd, in_=rstd), reads=["rstd"], writes=["rstd"])
        P.op("dve", lambda e: e.scalar_tensor_tensor(out=nmr, in0=mu, scalar=-1.0, in1=rstd, op0=ALU.mult, op1=ALU.mult),
             reads=["mu", "rstd"], writes=["nmr"])

    def stat_sq(j, ysrc, yname, sqrot):
        i = j % 2
        P.op("act", lambda e, o=sqrot[i], s=ysrc: e.activation(out=o.bitcast(F32R), in_=s, func=AF.Square), reads=[yname], writes=[f"sq{i}"])

    def stat_mm(j, ysrc, yname, sqrot, last, s1=6, s2=7):
        i = j % 2
        P.op("pe", lambda e, s=ysrc: e.matmul(ps[s1], lhsT=ones_r.bitcast(F32R), rhs=s.bitcast(F32R), start=(j == 0), stop=last),
             reads=[yname], writes=[f"ps{s1}"], signal=last)
        P.op("pe", lambda e, s=sqrot[i]: e.matmul(ps[s2], lhsT=ones_r.bitcast(F32R), rhs=s.bitcast(F32R), start=(j == 0), stop=last),
             reads=[f"sq{i}"], writes=[f"ps{s2}"], signal=True)

    class Defer:
        def __init__(self):
            self.f = None

        def flush(self):
            if self.f is not None:
                self.f()
                self.f = None

    def ln_out(yT, rstd, nmr, trot, orot, gkey, bkey, Xout, b):
        for j in range(KC):
            i = j % len(trot)
            P.op("dve", lambda e, o=trot[i], s=yT[:, j, :]: e.tensor_tensor(out=o, in0=s, in1=rstd, op=ALU.mult),
                 reads=[f"y{j}", "rstd"], writes=[f"trot{i}"])
            P.op("dve", lambda e, o=trot[i]: e.tensor_tensor(out=o, in0=o, in1=nmr, op=ALU.add),
                 reads=[f"trot{i}", "nmr"], writes=[f"trot{i}"])
            P.op("act", lambda e, o=orot[i], s=trot[i], j=j: e.activation(out=o, in_=s, func=AF.Identity, bias=vcol(bkey, j), scale=vcol(gkey, j)),
                 reads=[f"trot{i}"], writes=[f"orot{i}"])
            P.dma("pool", Xout[j * 128:(j + 1) * 128, b * TB:(b + 1) * TB], orot[i], f"orot{i}", reads=[f"orot{i}"])

    def ln_bufs():
        trot = [arena.take([128, TB], F32, 4) for _ in range(4)]
        orot = [arena.take([128, TB], F32, 4) for _ in range(4)]
        mu = arena.take([128, TB], F32, 4)
        rstd = arena.take([128, TB], F32, 4)
        nmr = arena.take([128, TB], F32, 4)
        return yT_g, sqrot_g, trot, orot, mu, rstd, nmr

    def phase_C(l, Xin, Xout):
        arena.reset()
        W1 = arena.take([128, KC, 2 * D], BF16, 2)
        W2 = arena.take([128, KC, D], BF16, 2)
        DG = arena.take([128, CW * 8, 128], BF16, 2)
        stg = [arena.take([128, 256], F32, 4) for _ in range(2)]
        xrot = [arena.take([128, TB], F32, 4) for _ in range(4)]
        xbs = [arena.take([128, KC, TB], BF16, 2) for _ in range(2)]
        hT = arena.take([128, KC, 30 + TB], BF16, 2)
        yT, sqrot, trot, orot, mu, rstd, nmr = ln_bufs()
        sg = trot[2:4]
        load_weights([(W1[:, :, 0:D], wd["a_pw1_w"][l][:, 0:D], D, D, "W1a"), (W1[:, :, D:2 * D], wd["a_pw1_w"][l][:, D:2 * D], D, D, "W1g")], stg)
        load_weights([(W2, wd["a_pw2_w"][l], D, D, "W2")], stg)
        for k in range(CW):
            for j in range(8):
                idx = k * 8 + j
                if idx % 2:
                    P.op("dve", lambda e, o=DG[:, idx, :], idx=idx: e.tensor_scalar(out=o, in0=idb, scalar1=vcol(("dww", l), idx), scalar2=None, op0=ALU.mult),
                         writes=[f"dg{idx}"])
                else:
                    P.op("act", lambda e, o=DG[:, idx, :], idx=idx: e.activation(out=o, in_=idb, func=AF.Identity, scale=vcol(("dww", l), idx)),
                         writes=[f"dg{idx}"])
        load_xb(Xin, 0, xrot, xbs[0], "xb0_")
        for b in range(NBLK):
            xb = xbs[b % 2]
            tag = f"xb{b % 2}_"
            if b + 1 < NBLK:
                load_xb(Xin, b + 1, xrot, xbs[(b + 1) % 2], f"xb{(b + 1) % 2}_")
            if b == 0:
                P.op("dve", lambda e: e.tensor_copy(out=hT[:, :, 0:30], in_=zeros32[:, 0:240].rearrange("p (a b) -> p a b", a=8)),
                     writes=[f"h{j}" for j in range(8)])
            else:
                P.op("dve", lambda e: e.tensor_copy(out=hT[:, :, 0:30], in_=hT[:, :, TB:TB + 30]),
                     reads=[f"h{j}" for j in range(8)], writes=[f"h{j}" for j in range(8)])
            for j in range(8):
                pa, pg = j % 2, 2 + j % 2
                for k in range(KC):
                    P.op("pe", lambda e, k=k, j=j, pa=pa, xb=xb: e.matmul(ps[pa], lhsT=W1[:, k, j * 128:(j + 1) * 128], rhs=xb[:, k, :], start=(k == 0), stop=(k == KC - 1)),
                         reads=[f"{tag}{k}", wr("W1a", k, j * 128)], writes=[f"ps{pa}"], signal=(k == KC - 1))
                for k in range(KC):
                    P.op("pe", lambda e, k=k, j=j, pg=pg, xb=xb: e.matmul(ps[pg], lhsT=W1[:, k, D + j * 128:D + (j + 1) * 128], rhs=xb[:, k, :], start=(k == 0), stop=(k == KC - 1)),
                         reads=[f"{tag}{k}", wr("W1g", k, j * 128)], writes=[f"ps{pg}"], signal=(k == KC - 1))
                P.op("act", lambda e, j=j, pg=pg: e.activation(out=sg[j % 2], in_=ps[pg], func=AF.Sigmoid, bias=vcol(("pw1b", l), 8 + j)),
                     reads=[f"ps{pg}"], writes=[f"trot{2 + j % 2}"])
                P.op("dve", lambda e, j=j, pa=pa: e.scalar_tensor_tensor(out=hT[:, j, 30:30 + TB], in0=ps[pa], scalar=vcol(("pw1b", l), j), in1=sg[j % 2], op0=ALU.add, op1=ALU.mult),
                     reads=[f"ps{pa}", f"trot{2 + j % 2}"], writes=[f"h{j}"])
            df = Defer()
            for j in range(8):
                pc = 4 + j % 2
                for k in range(CW):
                    P.op("pe", lambda e, k=k, j=j, pc=pc: e.matmul(ps[pc], lhsT=DG[:, k * 8 + j, :], rhs=hT[:, j, k:k + TB], start=(k == 0), stop=(k == CW - 1)),
                         reads=[f"h{j}", f"dg{k * 8 + j}"], writes=[f"ps{pc}"], signal=(k == CW - 1))
                df.flush()
                P.op("act", lambda e, j=j, pc=pc: e.activation(out=yT[:, j, :].bitcast(F32R), in_=ps[pc], func=AF.Identity, bias=vcol(("dwb", l), j)),
                     reads=[f"ps{pc}"], writes=[f"y{j}"])
                stat_sq(j, yT[:, j, :], f"y{j}", sqrot)
                df.f = (lambda j=j: stat_mm(j, yT[:, j, :], f"y{j}", sqrot, j == 7))
            df.flush()
            stats_finish(6, 7, mu, rstd, nmr)
            for j in range(8):
                i = j % 2
                P.op("dve", lambda e, o=trot[i], s=yT[:, j, :]: e.tensor_tensor(out=o, in0=s, in1=rstd, op=ALU.mult),
                     reads=[f"y{j}", "rstd"], writes=[f"trot{i}"])
                P.op("dve", lambda e, o=trot[i]: e.tensor_tensor(out=o, in0=o, in1=nmr, op=ALU.add),
                     reads=[f"trot{i}", "nmr"], writes=[f"trot{i}"])
                P.op("act", lambda e, o=xb[:, j, :], s=trot[i], j=j: e.activation(out=o, in_=s, func=AF.Silu, bias=vcol(("lnb", l), j), scale=vcol(("lng", l), j)),
                     reads=[f"trot{i}"], writes=[f"{tag}{j}"])
            for j in range(8):
                pm = 4 + j % 2
                i = j % 2
                for k in range(KC):
                    P.op("pe", lambda e, k=k, j=j, pm=pm, xb=xb: e.matmul(ps[pm], lhsT=W2[:, k, j * 128:(j + 1) * 128], rhs=xb[:, k, :], start=(k == 0), stop=(k == KC - 1)),
                         reads=[f"{tag}{k}", wr("W2", k, j * 128)], writes=[f"ps{pm}"], signal=(k == KC - 1))
                df.flush()
                P.op("act", lambda e, j=j, pm=pm, i=i: e.activation(out=sg[i], in_=ps[pm], func=AF.Identity, bias=vcol(("pw2b", l), j)),
                     reads=[f"ps{pm}"], writes=[f"trot{2 + i}"])
                P.dma("sp", xrot[2 + i], Xin[j * 128:(j + 1) * 128, b * TB:(b + 1) * TB], f"xrot{2 + i}", writes=[f"xrot{2 + i}"])
                P.op("dve", lambda e, j=j, i=i: e.scalar_tensor_tensor(out=yT[:, j, :].bitcast(F32R), in0=xrot[2 + i], scalar=ALPHA, in1=sg[i], op0=ALU.mult, op1=ALU.add),
                     reads=[f"xrot{2 + i}", f"trot{2 + i}"], writes=[f"y{j}"])
                stat_sq(j, yT[:, j, :], f"y{j}", sqrot)
                df.f = (lambda j=j: stat_mm(j, yT[:, j, :], f"y{j}", sqrot, j == 7))
            df.flush()
            stats_finish(6, 7, mu, rstd, nmr)
            ln_out(yT, rstd, nmr, trot, orot, ("mixg", l), ("mixb", l), Xout, b)
        P.barrier()

    def phase_F1(l, Xin):
        arena.reset()
        Wu = arena.take([128, KC, FF], BF16, 2)
        Wg = arena.take([128, KC, FF], BF16, 2)
        stg = [arena.take([128, 1024], F32, 4) for _ in range(4)]
        xrot = [arena.take([128, TB], F32, 4) for _ in range(2)]
        xbs = [arena.take([128, KC, TB], BF16, 2) for _ in range(2)]
        gs = [arena.take([128, TB + 2], F32, 4) for _ in range(3)]
        acc = [arena.take([128, TB], F32, 4) for _ in range(3)]
        hrot = [arena.take([128, TB], BF16, 2) for _ in range(3)]
        gprev = arena.take([128, FC, 2], F32, 4)
        load_weights([(Wu, wd["ffn_w_up"][l], D, FF, "Wu"), (Wg, wd["ffn_w_gate"][l], D, FF, "Wg")], stg)
        P.op("dve", lambda e: e.tensor_copy(out=gprev.rearrange("p a b -> p (a b)"), in_=zeros32[:, 0:2 * FC]), writes=["gprev"])
        load_xb(Xin, 0, xrot, xbs[0], "xb0_")
        cnt = 0
        for b in range(NBLK):
            xb = xbs[b % 2]
            tag = f"xb{b % 2}_"
            for f in range(FC):
                if b + 1 < NBLK and f <= KC:
                    if f < KC:
                        xb_dma(Xin, b + 1, xrot, f)
                    if f >= 1:
                        xb_cast(xrot, xbs[(b + 1) % 2], f"xb{(b + 1) % 2}_", f - 1)
                i = cnt % 3
                cnt += 1
                pu, pg = i, 3 + i
                for k in range(KC):
                    P.op("pe", lambda e, k=k, f=f, pu=pu, xb=xb: e.matmul(ps[pu], lhsT=Wu[:, k, f * 128:(f + 1) * 128], rhs=xb[:, k, :], start=(k == 0), stop=(k == KC - 1)),
                         reads=[f"{tag}{k}", wr("Wu", k, f * 128)], writes=[f"ps{pu}"], signal=(k == KC - 1))
                for k in range(KC):
                    P.op("pe", lambda e, k=k, f=f, pg=pg, xb=xb: e.matmul(ps[pg], lhsT=Wg[:, k, f * 128:(f + 1) * 128], rhs=xb[:, k, :], start=(k == 0), stop=(k == KC - 1)),
                         reads=[f"{tag}{k}", wr("Wg", k, f * 128)], writes=[f"ps{pg}"], signal=(k == KC - 1))
                P.op("act", lambda e, i=i, pg=pg: e.activation(out=gs[i][:, 2:2 + TB], in_=ps[pg], func=AF.Copy), reads=[f"ps{pg}"], writes=[f"gs{i}"])
                P.op("dve", lambda e, i=i, f=f: e.tensor_copy(out=gs[i][:, 0:2], in_=gprev[:, f, :]), reads=["gprev"], writes=[f"gs{i}"])
                P.op("act", lambda e, i=i, pg=pg, f=f: e.activation(out=acc[i], in_=ps[pg], func=AF.Identity, bias=vcol(("fcb", l), f), scale=vcol(("fcw", l), 2 * FC + f)),
                     reads=[f"ps{pg}"], writes=[f"acc{i}"])
                P.op("dve", lambda e, i=i, f=f: e.scalar_tensor_tensor(out=acc[i], in0=gs[i][:, 1:1 + TB], scalar=vcol(("fcw", l), FC + f), in1=acc[i], op0=ALU.mult, op1=ALU.add),
                     reads=[f"gs{i}", f"acc{i}"], writes=[f"acc{i}"])
                P.op("dve", lambda e, i=i, f=f: e.scalar_tensor_tensor(out=acc[i], in0=gs[i][:, 0:TB], scalar=vcol(("fcw", l), f), in1=acc[i], op0=ALU.mult, op1=ALU.add),
                     reads=[f"gs{i}", f"acc{i}"], writes=[f"acc{i}"])
                P.op("dve", lambda e, i=i, f=f: e.tensor_copy(out=gprev[:, f, :], in_=gs[i][:, TB:TB + 2]), reads=[f"gs{i}"], writes=["gprev"])
                P.op("act", lambda e, i=i: e.activation(out=acc[i], in_=acc[i], func=AF.Silu), reads=[f"acc{i}"], writes=[f"acc{i}"])
                P.op("dve", lambda e, i=i, pu=pu: e.tensor_tensor(out=hrot[i], in0=acc[i], in1=ps[pu], op=ALU.mult),
                     reads=[f"acc{i}", f"ps{pu}"], writes=[f"hrot{i}"])
                P.dma("pool", HH[f * 128:(f + 1) * 128, b * TB:(b + 1) * TB], hrot[i], f"hrot{i}", reads=[f"hrot{i}"])
        P.barrier()

    def phase_F2(l, Xin, Xout):
        arena.reset()
        Wd = arena.take([128, FC, D], BF16, 2)
        Wq = arena.take([128, KC, D], BF16, 2)
        Wp = arena.take([128, 2, D], BF16, 2)
        stg = [arena.take([128, 512], F32, 4) for _ in range(2)]
        xrot = [arena.take([128, TB], F32, 4) for _ in range(4)]
        xbs = [arena.take([128, KC, TB], BF16, 2) for _ in range(2)]
        hhs = [arena.take([128, FC, TB], BF16, 2) for _ in range(2)]
        p32 = arena.take([128, 2, TB], F32, 4)
        pbs = [arena.take([128, 2, TB], BF16, 2) for _ in range(2)]
        yT, sqrot, trot, orot, mu, rstd, nmr = ln_bufs()
        sgq = trot[0:2]
        load_weights([(Wd, wd["ffn_w_down"][l], FF, D, "Wd"), (Wq, wd["ple_w_gate"][l], D, D, "Wpg"), (Wp, wd["ple_w_proj"][l], PLE, D, "Wpp")], stg)

        def load_in(b):
            par = b % 2
            load_xb(Xin, b, xrot, xbs[par], f"xb{par}_")
            P.dma("sp", hhs[par][:, 0:11, :], blk(HH, b)[:, 0:11, :], f"hh{par}_0", writes=[f"hh{par}_0"])
            P.dma("sp", hhs[par][:, 11:22, :], blk(HH, b)[:, 11:22, :], f"hh{par}_1", writes=[f"hh{par}_1"])
            P.dma("sp", p32, blk(pT_d[l], b), "p32", writes=["p32"])
            P.op("pool", lambda e, o=pbs[par]: e.tensor_copy(out=o, in_=p32), reads=["p32"], writes=[f"pb{par}"])

        load_in(0)
        for b in range(NBLK):
            par = b % 2
            xb, hh, pb = xbs[par], hhs[par], pbs[par]
            tag = f"xb{par}_"
            if b + 1 < NBLK:
                load_in(b + 1)
            df = Defer()
            for j in range(8):
                i = j % 2
                pf, pq, pr = i, 2 + i, 4 + i
                for f in range(FC):
                    P.op("pe", lambda e, f=f, j=j, pf=pf, hh=hh: e.matmul(ps[pf], lhsT=Wd[:, f, j * 128:(j + 1) * 128], rhs=hh[:, f, :], start=(f == 0), stop=(f == FC - 1)),
                         reads=[f"hh{par}_{f // 11}", wr("Wd", f, j * 128)], writes=[f"ps{pf}"], signal=(f == FC - 1))
                for k in range(KC):
                    P.op("pe", lambda e, k=k, j=j, pq=pq, xb=xb: e.matmul(ps[pq], lhsT=Wq[:, k, j * 128:(j + 1) * 128], rhs=xb[:, k, :], start=(k == 0), stop=(k == KC - 1)),
                         reads=[f"{tag}{k}", wr("Wpg", k, j * 128)], writes=[f"ps{pq}"], signal=(k == KC - 1))
                for c in range(2):
                    P.op("pe", lambda e, c=c, j=j, pr=pr, pb=pb: e.matmul(ps[pr], lhsT=Wp[:, c, j * 128:(j + 1) * 128], rhs=pb[:, c, :], start=(c == 0), stop=(c == 1)),
                         reads=[f"pb{par}", wr("Wpp", c, j * 128)], writes=[f"ps{pr}"], signal=(c == 1))
                df.flush()
                P.op("act", lambda e, i=i, pq=pq: e.activation(out=sgq[i], in_=ps[pq], func=AF.Sigmoid), reads=[f"ps{pq}"], writes=[f"trot{i}"])
                P.op("dve", lambda e, i=i, pr=pr: e.tensor_tensor(out=sgq[i], in0=sgq[i], in1=ps[pr], op=ALU.mult), reads=[f"trot{i}", f"ps{pr}"], writes=[f"trot{i}"])
                P.op("dve", lambda e, i=i, pf=pf: e.tensor_tensor(out=sgq[i], in0=sgq[i], in1=ps[pf], op=ALU.add), reads=[f"trot{i}", f"ps{pf}"], writes=[f"trot{i}"])
                P.dma("sp", xrot[2 + i], Xin[j * 128:(j + 1) * 128, b * TB:(b + 1) * TB], f"xrot{2 + i}", writes=[f"xrot{2 + i}"])
                P.op("dve", lambda e, i=i, j=j: e.scalar_tensor_tensor(out=yT[:, j, :].bitcast(F32R), in0=xrot[2 + i], scalar=ALPHA, in1=sgq[i], op0=ALU.mult, op1=ALU.add),
                     reads=[f"xrot{2 + i}", f"trot{i}"], writes=[f"y{j}"])
                stat_sq(j, yT[:, j, :], f"y{j}", sqrot)
                df.f = (lambda j=j: stat_mm(j, yT[:, j, :], f"y{j}", sqrot, j == 7))
            df.flush()
            stats_finish(6, 7, mu, rstd, nmr)
            ln_out(yT, rstd, nmr, trot, orot, ("ffg", l), ("ffb", l), Xout, b)
        P.barrier()

    def phase_KVQ(Xin, lq, do_kv):
        arena.reset()
        Wq = arena.take([128, KC, D], BF16, 2)
        if do_kv:
            Wk = arena.take([128, KC, D], BF16, 2)
            Wv = arena.take([128, KC, D], BF16, 2)
        stg = [arena.take([128, 1024], F32, 4) for _ in range(4)]
        xrot = [arena.take([128, TB], F32, 4) for _ in range(2)]
        xbs = [arena.take([128, KC, TB], BF16, 2) for _ in range(2)]
        krot = [arena.take([128, TB], BF16, 2) for _ in range(3)]
        load_weights([(Wq, wd["b_wq"][lq], D, D, "Wq")], stg)
        if do_kv:
            load_weights([(Wk, wd["kv_wk"], D, D, "Wk")], stg)
            load_weights([(Wv, wd["kv_wv"], D, D, "Wv")], stg)
        cnt = 0
        load_xb(Xin, 0, xrot, xbs[0], "xb0_")
        for b in range(NBLK):
            xb = xbs[b % 2]
            tag = f"xb{b % 2}_"
            if b + 1 < NBLK:
                load_xb(Xin, b + 1, xrot, xbs[(b + 1) % 2], f"xb{(b + 1) % 2}_")
            for W, dst, scale, wt in ([(Wq, QT, 0.125, "Wq")] + ([(Wk, KT, 1.0, "Wk")] if do_kv else [])):
                for j in range(8):
                    i = cnt % 3
                    cnt += 1
                    for k in range(KC):
                        P.op("pe", lambda e, k=k, j=j, i=i, W=W, xb=xb: e.matmul(ps[i], lhsT=W[:, k, j * 128:(j + 1) * 128], rhs=xb[:, k, :], start=(k == 0), stop=(k == KC - 1)),
                             reads=[f"{tag}{k}", wr(wt, k, j * 128)], writes=[f"ps{i}"], signal=(k == KC - 1))
                    P.op("act", lambda e, i=i, scale=scale: e.activation(out=krot[i], in_=ps[i], func=AF.Copy, scale=scale), reads=[f"ps{i}"], writes=[f"krot{i}"])
                    P.dma("pool", dst[j * 128:(j + 1) * 128, b * TB:(b + 1) * TB], krot[i], f"krot{i}", reads=[f"krot{i}"])
            if do_kv:
                for t in range(4):
                    for hf in range(2):
                        i = cnt % 3
                        cnt += 1
                        for k in range(KC):
                            P.op("pe", lambda e, k=k, t=t, hf=hf, i=i, xb=xb: e.matmul(ps[i], lhsT=xb[:, k, t * 128:(t + 1) * 128], rhs=Wv[:, k, hf * 512:(hf + 1) * 512], start=(k == 0), stop=(k == KC - 1)),
                                 reads=[f"{tag}{k}", wr("Wv", k, hf * 512)], writes=[f"ps{i}"], signal=(k == KC - 1))
                        P.op("dve", lambda e, i=i: e.tensor_copy(out=krot[i], in_=ps[i]), reads=[f"ps{i}"], writes=[f"krot{i}"])
                        P.dma("pool", VV[b * TB + t * 128:b * TB + (t + 1) * 128, hf * 512:(hf + 1) * 512], krot[i], f"krot{i}", reads=[f"krot{i}"])
        P.barrier()

    def phase_ATT():
        arena.reset()
        Kh = [arena.take([128, S], BF16, 2) for _ in range(2)]
        Qh = [arena.take([128, S], BF16, 2) for _ in range(2)]
        Vh = [arena.take([128, S // 128, 128], BF16, 2) for _ in range(2)]
        for t in (Kh, Qh):
            for i in range(2):
                for c in range(S // 512):
                    P.op("dve", lambda e, o=t[i][64:128, c * 512:(c + 1) * 512]: e.tensor_copy(out=o, in_=zeros32[64:128, :]), writes=["zpad"])
        for i in range(2):
            for c in range(4):
                P.op("pool", lambda e, o=Vh[i][:, c * 8:(c + 1) * 8, 64:128]: e.tensor_copy(out=o, in_=zeros32.rearrange("p (a b) -> p a b", a=8)), writes=["zpad2"])
        mstage = arena.take([128, 2048], F32, 4)
        mask16 = arena.take([128, 4, 512], F16, 2)
        maskb = arena.take([128, 4, 512], BF16, 2)
        P.dma("sp", mstage, consts_d[:, C_MASK:C_MASK + 2048], "mstage", writes=["mstage"])
        P.op("dve", lambda e: e.tensor_copy(out=mask16.rearrange("p a b -> p (a b)"), in_=mstage), reads=["mstage"], writes=["k4"])
        P.op("act", lambda e: e.activation(out=maskb.rearrange("p a b -> p (a b)"), in_=mstage, func=AF.Copy), reads=["mstage"], writes=["k5"])
        P.barrier()
        e32p = [arena.take([128, 2 * TB], F32, 4) for _ in range(3)]
        ec32 = [arena.take([128, TB], F32, 4) for _ in range(2)]
        sp16p = [arena.take([128, 2 * TB], F16, 2) for _ in range(3)]
        S16 = [arena.take([128, TB], F16, 2) for _ in range(2)]
        a16 = [arena.take([128, TB], BF16, 2) for _ in range(4)]
        orot = [arena.take([64, TB], BF16, 2) for _ in range(2)]
        items = []
        for h in range(NH):
            for qb in range(NBLK):
                kts = list(range(qb * 4 + 3, -1, -1))
                for n, kt in enumerate(kts):
                    items.append((h, qb, kt, n == 0, n == len(kts) - 1, kt - qb * 4))
        NI = len(items)
        NP = NI // 2

        def load_head(h):
            hi = h % 2
            P.dma("sp", Kh[hi][0:DH, :], KT[h * DH:(h + 1) * DH, :], f"Kh{hi}", writes=[f"Kh{hi}"])
            P.dma("sp", Qh[hi][0:DH, :], QT[h * DH:(h + 1) * DH, :], f"Qh{hi}", writes=[f"Qh{hi}"])
            for c in range(4):
                P.dma("sp", Vh[hi][:, c * 8:(c + 1) * 8, 0:DH], VV.rearrange("(t p) d -> p t d", p=128)[:, c * 8:(c + 1) * 8, h * DH:(h + 1) * DH],
                      f"Vh{hi}_{c}", writes=[f"Vh{hi}_{c}"])

        st = {"si": 0}

        def stage_A(p):
            pb, eb = p % 2, p % 3
            for u in range(2):
                h, qb, kt, first, last, d = items[2 * p + u]
                hi = h % 2
                P.op("pe", lambda e, hi=hi, kt=kt, qb=qb, u=u: e.matmul(pp[pb][:, u * TB:(u + 1) * TB], lhsT=Kh[hi][:, kt * 128:(kt + 1) * 128], rhs=Qh[hi][:, qb * TB:(qb + 1) * TB], start=True, stop=True),
                     reads=[f"Kh{hi}", f"Qh{hi}"], writes=[f"pz{pb}"], signal=True)
            P.op("act", lambda e: e.activation(out=e32p[eb], in_=pp[pb], func=AF.Exp), reads=[f"pz{pb}"], writes=[f"e{eb}"])
            P.op("act", lambda e: e.activation(out=sp16p[eb], in_=e32p[eb], func=AF.Ln, bias=1.0), reads=[f"e{eb}"], writes=[f"sp{eb}"])
            for u in range(2):
                d = items[2 * p + u][5]
                if d >= 0:
                    P.op("pool", lambda e, u=u, d=d: e.tensor_tensor(out=sp16p[eb][:, u * TB:(u + 1) * TB], in0=sp16p[eb][:, u * TB:(u + 1) * TB], in1=mask16[:, d, :], op=ALU.mult),
                         reads=[f"sp{eb}"], writes=[f"sp{eb}"])

        def stage_B(t):
            h, qb, kt, first, last, d = items[t]
            eb, u = (t // 2) % 3, t % 2
            e32 = e32p[eb][:, u * TB:(u + 1) * TB]
            sp16 = sp16p[eb][:, u * TB:(u + 1) * TB]
            i4, j = t % 4, t % 2
            pc = 4 + j
            si = st["si"]
            P.op("pe", lambda e: e.matmul(ps[pc], lhsT=tri16, rhs=sp16, start=True, stop=first),
                 reads=[f"sp{eb}"], writes=[f"ps{pc}"], signal=first)
            if not first:
                P.op("pe", lambda e: e.matmul(ps[pc], lhsT=ones16, rhs=S16[si], start=False, stop=True),
                     reads=[f"S{si}"], writes=[f"ps{pc}"], signal=True)
            if not last:
                if first:
                    P.op("dve", lambda e: e.tensor_copy(out=S16[0], in_=sp16), reads=[f"sp{eb}"], writes=["S0"])
                    st["si"] = 0
                else:
                    P.op("dve", lambda e: e.tensor_tensor(out=S16[1 - si], in0=S16[si], in1=sp16, op=ALU.add),
                         reads=[f"sp{eb}", f"S{si}"], writes=[f"S{1 - si}"])
                    st["si"] = 1 - si
            P.op("act", lambda e: e.activation(out=ec32[j], in_=ps[pc], func=AF.Exp, scale=-1.0), reads=[f"ps{pc}"], writes=[f"ec{j}"])
            P.op("dve", lambda e: e.tensor_tensor(out=a16[i4], in0=e32, in1=ec32[j], op=ALU.mult), reads=[f"e{eb}", f"ec{j}"], writes=[f"a{i4}"])
            if d >= 0:
                P.op("pool", lambda e: e.tensor_tensor(out=a16[i4], in0=a16[i4], in1=maskb[:, d, :], op=ALU.mult), reads=[f"a{i4}"], writes=[f"a{i4}"])

        def stage_C(t):
            h, qb, kt, first, last, d = items[t]
            hi, i4 = h % 2, t % 4
            hq = h * NBLK + qb
            po = 6 + hq % 2
            P.op("pe", lambda e: e.matmul(ps[po], lhsT=Vh[hi][:, kt, :], rhs=a16[i4], start=first, stop=last),
                 reads=[f"a{i4}", f"Vh{hi}_{kt // 8}"], writes=[f"ps{po}"], signal=last)
            if last:
                oi = hq % 2
                P.op("dve", lambda e: e.tensor_copy(out=orot[oi], in_=ps[po][0:DH, :]), reads=[f"ps{po}"], writes=[f"orot{oi}"])
                P.dma("pool", OT[h * DH:(h + 1) * DH, qb * TB:(qb + 1) * TB], orot[oi], f"orot{oi}", reads=[f"orot{oi}"])
                if qb == NBLK - 1 and h + 2 < NH:
                    load_head(h + 2)

        load_head(0)
        load_head(1)
        for p in range(NP + 2):
            if p < NP:
                stage_A(p)
            if 0 <= p - 1 < NP:
                stage_B(2 * (p - 1))
                stage_B(2 * (p - 1) + 1)
            if 0 <= p - 2 < NP:
                stage_C(2 * (p - 2))
                stage_C(2 * (p - 2) + 1)
        P.barrier()

    def phase_WO(l, Xin, Xout):
        arena.reset()
        Wo = arena.take([128, KC, D], BF16, 2)
        stg = [arena.take([128, 1024], F32, 4) for _ in range(4)]
        xrot = [arena.take([128, TB], F32, 4) for _ in range(4)]
        xbs = [arena.take([128, KC, TB], BF16, 2) for _ in range(2)]
        yT, sqrot, trot, orot, mu, rstd, nmr = ln_bufs()
        load_weights([(Wo, wd["b_wo"][l - NA], D, D, "Wo")], stg)
        P.dma("sp", xbs[0], blk(OT, 0), "xbfull0", writes=[f"xb0_{k}" for k in range(KC)])
        for b in range(NBLK):
            xb = xbs[b % 2]
            tag = f"xb{b % 2}_"
            if b + 1 < NBLK:
                P.dma("sp", xbs[(b + 1) % 2], blk(OT, b + 1), f"xbfull{(b + 1) % 2}", writes=[f"xb{(b + 1) % 2}_{k}" for k in range(KC)])
            df = Defer()
            for j in range(8):
                pm = 4 + j % 2
                i = j % 2
                for k in range(KC):
                    P.op("pe", lambda e, k=k, j=j, pm=pm, xb=xb: e.matmul(ps[pm], lhsT=Wo[:, k, j * 128:(j + 1) * 128], rhs=xb[:, k, :], start=(k == 0), stop=(k == KC - 1)),
                         reads=[f"{tag}{k}", wr("Wo", k, j * 128)], writes=[f"ps{pm}"], signal=(k == KC - 1))
                df.flush()
                P.dma("sp", xrot[2 + i], Xin[j * 128:(j + 1) * 128, b * TB:(b + 1) * TB], f"xrot{2 + i}", writes=[f"xrot{2 + i}"])
                P.op("dve", lambda e, j=j, i=i, pm=pm: e.scalar_tensor_tensor(out=yT[:, j, :].bitcast(F32R), in0=xrot[2 + i], scalar=ALPHA, in1=ps[pm], op0=ALU.mult, op1=ALU.add),
                     reads=[f"xrot{2 + i}", f"ps{pm}"], writes=[f"y{j}"])
                stat_sq(j, yT[:, j, :], f"y{j}", sqrot)
                df.f = (lambda j=j: stat_mm(j, yT[:, j, :], f"y{j}", sqrot, j == 7))
            df.flush()
            stats_finish(6, 7, mu, rstd, nmr)
            ln_out(yT, rstd, nmr, trot, orot, ("mixg", l), ("mixb", l), Xout, b)
        P.barrier()

    cur = xT_d
    for l in range(nlayers):
        last = (l == nlayers - 1)
        if l < NA:
            phase_C(l, cur, XA)
        else:
            phase_KVQ(cur, l - NA, do_kv=(l == NA))
            phase_ATT()
            phase_WO(l, cur, XA)
        phase_F1(l, XA)
        nxt = out_d if last else XB
        phase_F2(l, XA, nxt)
        cur = nxt
    with nc.Block() as block:
        P.replay(block)
    return nc


_NC_CACHE = {}


def kernel(**inputs):
    inp = {k: np.asarray(v) for k, v in inputs.items()}
    if "nc" not in _NC_CACHE:
        _NC_CACHE["nc"] = build(DEPTH)
    nc = _NC_CACHE["nc"]
    vecs = host_vecs(inp)
    consts = host_consts()
    shared = {"vecs": vecs, "consts": consts}
    for name in ("a_pw1_w", "a_pw2_w", "b_wq", "kv_wk", "kv_wv", "b_wo", "ffn_w_up", "ffn_w_gate",
                 "ffn_w_down", "ple_w_gate", "ple_w_proj"):
        shared[name] = np.ascontiguousarray(inp[name], dtype=np.float32)
    in_maps = []
    for c in range(NCORES):
        m = dict(shared)
        m["xT"] = np.ascontiguousarray(inp["x"][c].T)
        m["pT"] = np.ascontiguousarray(np.transpose(inp["p"][:, c], (0, 2, 1)))
        in_maps.append(m)
    res = run_bass_kernel_spmd(nc, in_maps, core_ids=list(range(NCORES)))
    out = np.stack([np.asarray(r["outT"]).T for r in res.results], axis=0)
    return np.ascontiguousarray(out.astype(np.float32))
```
